# Optimizing a Trainium2 kernel written in Bass

```python
import jax, jax.numpy as jnp
from jax import lax
import numpy as np

D_MODEL = 1024
BATCH = 32
SEQ = 256
DEPTH = 2
DEC_BATCH = 8
DEC_SEQ = 2048
PAST_LEN = 512

GRID_W = 64
N_HEADS = 8
N_KV_HEADS = 2
HEAD_DIM = 128
KV_REP = N_HEADS // N_KV_HEADS
ATTN_W = N_HEADS * HEAD_DIM
KV_W = N_KV_HEADS * HEAD_DIM
ROPE_THETA = 10000.0
Q_BLOCK = 128
CONV_W = D_MODEL
CONV_K = 3
D_INNER = 2 * D_MODEL
SSD_HEADDIM = 64
N_SSD_HEADS = D_INNER // SSD_HEADDIM
N_SSD_GROUPS = 8
SSD_HPG = N_SSD_HEADS // N_SSD_GROUPS
D_STATE = 128
SSD_CONV_DIM = D_INNER + 2 * N_SSD_GROUPS * D_STATE
CHUNK = 128
D_FF = -(-8 * D_MODEL // 768) * 256
N_BRANCH = 3
IN_SIZES = (ATTN_W, KV_W, KV_W, CONV_W, CONV_W, CONV_W, D_INNER, SSD_CONV_DIM, 2 * N_SSD_HEADS, N_BRANCH * D_MODEL)
IN_DIM = sum(IN_SIZES)
EPS = 1e-6

kernel_name = 'hybrid_gated_attn_conv_ssd_diffusion_step'


def rms_norm(x, g):
    xf = x.astype(jnp.float32)
    y = xf * lax.rsqrt(jnp.mean(jnp.square(xf), axis=-1, keepdims=True) + EPS)
    return (y * g.astype(jnp.float32)).astype(x.dtype)


def dwconv_centred(x, w):
    return lax.conv_general_dilated(x, w[:, None, :].astype(x.dtype), window_strides=(1,),
                                    padding=[(CONV_K // 2, CONV_K // 2)],
                                    dimension_numbers=('NWC', 'WIO', 'NWC'),
                                    feature_group_count=x.shape[-1])


def axial_rope(n_tok):
    rows = n_tok // GRID_W
    row = jnp.repeat(jnp.arange(rows, dtype=jnp.float32), GRID_W)
    col = jnp.tile(jnp.arange(GRID_W, dtype=jnp.float32), rows)
    axis_dim = HEAD_DIM // 2
    inv_freq = 1.0 / (ROPE_THETA ** (jnp.arange(0, axis_dim, 2, dtype=jnp.float32) / axis_dim))
    ang = jnp.concatenate([row[:, None] * inv_freq, col[:, None] * inv_freq], axis=-1)
    return jnp.cos(ang), jnp.sin(ang)


def apply_rope(x, cos, sin):
    xp = x.reshape(x.shape[:-1] + (HEAD_DIM // 2, 2))
    xe, xo = xp[..., 0], xp[..., 1]
    cs, sn = cos[None, :, None, :], sin[None, :, None, :]
    return jnp.stack([xe * cs - xo * sn, xe * sn + xo * cs], axis=-1).reshape(x.shape)


def blocked_attention(q, keys, vals):
    b, t = q.shape[:2]
    nblk = t // Q_BLOCK
    qb = q.reshape(b, nblk, Q_BLOCK, N_KV_HEADS, KV_REP, HEAD_DIM).transpose(1, 0, 2, 3, 4, 5)
    scale = HEAD_DIM ** -0.5

    def one_block(qblk):
        s = jnp.einsum('bqgrd,bsgd->bgrqs', qblk, keys) * scale
        p = jax.nn.softmax(s, axis=-1)
        return jnp.einsum('bgrqs,bsgd->bqgrd', p, vals)

    out = lax.map(one_block, qb)
    return out.transpose(1, 0, 2, 3, 4, 5).reshape(b, t, ATTN_W)


def ssd_chunked(x, dt, a, bm, cm, s0):
    b, l = x.shape[:2]
    nc = l // CHUNK
    g, r, p, n = N_SSD_GROUPS, SSD_HPG, SSD_HEADDIM, D_STATE
    acs = jnp.cumsum((dt * a).reshape(b, nc, CHUNK, g, r), axis=2)
    xd = (x * dt[..., None]).reshape(b, nc, CHUNK, g, r, p)
    bc = bm.reshape(b, nc, CHUNK, g, n)
    cc = cm.reshape(b, nc, CHUNK, g, n)
    seg = acs[:, :, :, None] - acs[:, :, None]
    causal = jnp.tril(jnp.ones((CHUNK, CHUNK), dtype=bool))[:, :, None, None]
    decay = jnp.exp(jnp.where(causal, seg, -jnp.inf))
    cb = jnp.einsum('bcign,bcjgn->bcijg', cc, bc)
    y_diag = jnp.einsum('bcijgr,bcjgrp->bcigrp', cb[..., None] * decay, xd)
    to_end = jnp.exp(acs[:, :, -1:] - acs)
    states = jnp.einsum('bcjgn,bcjgrp->bcgrpn', bc, xd * to_end[..., None])
    chunk_decay = jnp.exp(acs[:, :, -1])

    def step(s, inp):
        st, dec = inp
        return s * dec[..., None, None] + st, s

    final, prev = lax.scan(step, s0, (states.transpose(1, 0, 2, 3, 4, 5), chunk_decay.transpose(1, 0, 2, 3)))
    prev = prev.transpose(1, 0, 2, 3, 4, 5)
    y_off = jnp.einsum('bcign,bcgrpn->bcigrp', cc, prev) * jnp.exp(acs)[..., None]
    return (y_diag + y_off).reshape(b, l, g, r, p), final


def token_mixers(h, latent, ctx_k, ctx_v, ssd_s0, w_in, q_g, k_g, w_attn_o, conv_w, w_conv_o,
                 ssd_conv_w, ssd_conv_b, ssd_dt_bias, ssd_a_log, ssd_d, ssd_norm_g, w_ssd_o, w_merge):
    f32 = jnp.float32
    b, t, _ = h.shape
    proj = h @ w_in
    (q, k, v, conv_bg, conv_cg, conv_x, z, xbc, dt_raw, gate_logits) = jnp.split(
        proj, np.cumsum(IN_SIZES)[:-1].tolist(), axis=-1)
    q = rms_norm(q.reshape(b, t, N_HEADS, HEAD_DIM), q_g).astype(f32)
    k = rms_norm(k.reshape(b, t, N_KV_HEADS, HEAD_DIM), k_g)
    v = v.reshape(b, t, N_KV_HEADS, HEAD_DIM)
    if latent:
        cos, sin = axial_rope(t)
        q = apply_rope(q, cos, sin)
        keys = jnp.concatenate([apply_rope(k.astype(f32), cos, sin), ctx_k.astype(f32)], axis=1)
        vals = jnp.concatenate([v.astype(f32), ctx_v.astype(f32)], axis=1)
    else:
        keys, vals = k.astype(f32), v.astype(f32)
    y_attn = blocked_attention(q, keys, vals).astype(h.dtype) @ w_attn_o
    y_conv = (conv_bg * dwconv_centred(conv_cg * conv_x, conv_w)) @ w_conv_o
    xbc = jax.nn.silu(dwconv_centred(xbc, ssd_conv_w) + ssd_conv_b)
    xs, bm, cm = jnp.split(xbc, [D_INNER, D_INNER + N_SSD_GROUPS * D_STATE], axis=-1)
    xs = xs.reshape(b, t, N_SSD_GROUPS, SSD_HPG, SSD_HEADDIM).astype(f32)
    bm = bm.reshape(b, t, N_SSD_GROUPS, D_STATE).astype(f32)
    cm = cm.reshape(b, t, N_SSD_GROUPS, D_STATE).astype(f32)
    dt = jax.nn.softplus(dt_raw.astype(f32).reshape(b, t, 2, N_SSD_GROUPS, SSD_HPG)
                         + ssd_dt_bias.astype(f32).reshape(2, N_SSD_GROUPS, SSD_HPG))
    a = -jnp.exp(ssd_a_log.astype(f32)).reshape(2, N_SSD_GROUPS, SSD_HPG)
    s0 = ssd_s0.astype(f32).reshape(b, 2, N_SSD_GROUPS, SSD_HPG, SSD_HEADDIM, D_STATE)
    y_f, s_f = ssd_chunked(xs, dt[:, :, 0], a[0], bm, cm, s0[:, 0])
    y_b, s_b = ssd_chunked(xs[:, ::-1], dt[:, ::-1, 1], a[1], bm[:, ::-1], cm[:, ::-1], s0[:, 1])
    y = y_f + y_b[:, ::-1] + ssd_d.astype(f32).reshape(N_SSD_GROUPS, SSD_HPG, 1) * xs
    y = y.reshape(b, t, D_INNER).astype(h.dtype)
    y_ssd = rms_norm(y * jax.nn.silu(z), ssd_norm_g) @ w_ssd_o
    g_attn, g_conv, g_ssd = jnp.split(jax.nn.sigmoid(gate_logits), N_BRANCH, axis=-1)
    out = (g_attn * y_attn + g_conv * y_conv + g_ssd * y_ssd) @ w_merge
    if latent:
        return out, None, None, None
    s_final = jnp.stack([s_f, s_b], axis=1).reshape(b, 2, N_SSD_HEADS, SSD_HEADDIM, D_STATE)
    return out, k, v, s_final


def trunk_layer(x, mod, latent, ctx_k, ctx_v, ssd_s0, norm1_g, norm2_g, w_in, q_g, k_g, w_attn_o, conv_w,
                w_conv_o, ssd_conv_w, ssd_conv_b, ssd_dt_bias, ssd_a_log, ssd_d, ssd_norm_g, w_ssd_o,
                w_merge, ffn_w1, ffn_w2):
    shift1, scale1, gate1, shift2, scale2, gate2 = jnp.split(mod, 6, axis=-1)
    h = rms_norm(x, norm1_g) * (1.0 + scale1) + shift1
    mixed, k, v, s = token_mixers(h, latent, ctx_k, ctx_v, ssd_s0, w_in, q_g, k_g, w_attn_o, conv_w,
                                  w_conv_o, ssd_conv_w, ssd_conv_b, ssd_dt_bias, ssd_a_log, ssd_d,
                                  ssd_norm_g, w_ssd_o, w_merge)
    x = x + gate1 * mixed
    h = rms_norm(x, norm2_g) * (1.0 + scale2) + shift2
    hg, hu = jnp.split(h @ ffn_w1, 2, axis=-1)
    x = x + gate2 * ((jax.nn.silu(hg) * hu) @ ffn_w2)
    return x, k, v, s


def setup_inputs(seed: int = 0) -> dict:
    key = jax.random.key(seed)
    ks = iter(jax.random.split(key, 40))
    f32 = jnp.float32

    def nrm(shape, scale):
        return jax.random.normal(next(ks), shape, f32) * scale

    dt0 = jnp.exp(jax.random.uniform(next(ks), (DEPTH, 2, N_SSD_HEADS), f32, np.log(1e-3), np.log(1e-1)))
    return {
        'x_prompt': nrm((BATCH, SEQ, D_MODEL), 1.0),
        'x_sample': nrm((DEC_BATCH, DEC_SEQ, D_MODEL), 1.0),
        'c': nrm((DEC_BATCH, D_MODEL), 1.0),
        'cache_k': nrm((DEC_BATCH, DEPTH, PAST_LEN, N_KV_HEADS, HEAD_DIM), 1.0),
        'cache_v': nrm((DEC_BATCH, DEPTH, PAST_LEN, N_KV_HEADS, HEAD_DIM), 1.0),
        'state_ssd': nrm((DEC_BATCH, DEPTH, 2, N_SSD_HEADS, SSD_HEADDIM, D_STATE), 0.1),
        'c_ctx': nrm((D_MODEL,), 1.0),
        'ada_w': nrm((DEPTH, D_MODEL, 6 * D_MODEL), 0.5 * D_MODEL ** -0.5),
        'ada_b': nrm((DEPTH, 6 * D_MODEL), 0.02),
        'norm1_g': 1.0 + nrm((DEPTH, D_MODEL), 0.02),
        'norm2_g': 1.0 + nrm((DEPTH, D_MODEL), 0.02),
        'w_in': nrm((DEPTH, D_MODEL, IN_DIM), D_MODEL ** -0.5),
        'q_norm_g': 1.0 + nrm((DEPTH, HEAD_DIM), 0.02),
        'k_norm_g': 1.0 + nrm((DEPTH, HEAD_DIM), 0.02),
        'w_attn_o': nrm((DEPTH, ATTN_W, D_MODEL), ATTN_W ** -0.5),
        'conv_w': nrm((DEPTH, CONV_K, CONV_W), CONV_K ** -0.5),
        'w_conv_o': nrm((DEPTH, CONV_W, D_MODEL), CONV_W ** -0.5),
        'ssd_conv_w': nrm((DEPTH, CONV_K, SSD_CONV_DIM), CONV_K ** -0.5),
        'ssd_conv_b': nrm((DEPTH, SSD_CONV_DIM), 0.02),
        'ssd_dt_bias': dt0 + jnp.log(-jnp.expm1(-dt0)),
        'ssd_a_log': jnp.log(jax.random.uniform(next(ks), (DEPTH, 2, N_SSD_HEADS), f32, 1.0, 16.0)),
        'ssd_d': 1.0 + nrm((DEPTH, N_SSD_HEADS), 0.1),
        'ssd_norm_g': 1.0 + nrm((DEPTH, D_INNER), 0.02),
        'w_ssd_o': nrm((DEPTH, D_INNER, D_MODEL), D_INNER ** -0.5),
        'w_merge': nrm((DEPTH, D_MODEL, D_MODEL), D_MODEL ** -0.5),
        'ffn_w1': nrm((DEPTH, D_MODEL, 2 * D_FF), D_MODEL ** -0.5),
        'ffn_w2': nrm((DEPTH, D_FF, D_MODEL), D_FF ** -0.5),
    }


def reference(x_prompt, x_sample, c, cache_k, cache_v, state_ssd, c_ctx, ada_w, ada_b, norm1_g, norm2_g,
              w_in, q_norm_g, k_norm_g, w_attn_o, conv_w, w_conv_o, ssd_conv_w, ssd_conv_b, ssd_dt_bias,
              ssd_a_log, ssd_d, ssd_norm_g, w_ssd_o, w_merge, ffn_w1, ffn_w2):
    y_prompt, y_sample = x_prompt, x_sample
    silu_ctx = jax.nn.silu(c_ctx)[None, :]
    silu_c = jax.nn.silu(c)
    zero_state = jnp.zeros((x_prompt.shape[0], 2, N_SSD_HEADS, SSD_HEADDIM, D_STATE), jnp.float32)
    ks_out, vs_out, ss_out = [], [], []
    for l in range(DEPTH):
        mod_ctx = (silu_ctx @ ada_w[l] + ada_b[l])[:, None, :]
        mod_lat = (silu_c @ ada_w[l] + ada_b[l])[:, None, :]
        weights = (norm1_g[l], norm2_g[l], w_in[l], q_norm_g[l], k_norm_g[l], w_attn_o[l], conv_w[l],
                   w_conv_o[l], ssd_conv_w[l], ssd_conv_b[l], ssd_dt_bias[l], ssd_a_log[l], ssd_d[l],
                   ssd_norm_g[l], w_ssd_o[l], w_merge[l], ffn_w1[l], ffn_w2[l])
        y_prompt, k_l, v_l, s_l = trunk_layer(y_prompt, mod_ctx, False, None, None, zero_state, *weights)
        y_sample, _, _, _ = trunk_layer(y_sample, mod_lat, True, cache_k[:, l], cache_v[:, l],
                                        state_ssd[:, l], *weights)
        ks_out.append(k_l)
        vs_out.append(v_l)
        ss_out.append(s_l)
    new_k = jnp.stack(ks_out, axis=1)
    new_v = jnp.stack(vs_out, axis=1)
    new_ssd = jnp.stack(ss_out, axis=1)
    return (y_prompt, y_sample, new_k, new_v, new_ssd)
```

```python
from contextlib import ExitStack
import numpy as np
import concourse.bass as bass
import concourse.mybir as mybir
from concourse.bass_utils import run_bass_kernel_spmd

F32 = mybir.dt.float32
F32R = mybir.dt.float32r
AF = mybir.ActivationFunctionType
ALU = mybir.AluOpType
AX = mybir.AxisListType

D = 1024
NL = 2
TP = 256
TS = 2048
PAST = 512
NTOK = 4 * TP + TS
IN_DIM = 13888
DFF = 2816
EPS = 1e-6
C_Q, C_K, C_V, C_BG, C_CG, C_CX, C_Z, C_XBC, C_DT, C_GT = 0, 1024, 1280, 1536, 2560, 3584, 4608, 6656, 10752, 10816
PT_W = 3648
FM_CONV, FM_XBC, FM_GT = 0, 3072, 7168
FM_ROWS = 10240
SEQS = [(0, 256, False), (256, 256, False), (512, 256, False), (768, 256, False), (1024, 2048, True)]
GROUPS = [(0, 1024, [0, 1, 2, 3]), (1024, 2048, [4])]


class Op:
    __slots__ = ("eng", "fn", "reads", "writes", "dsem", "deps", "signals", "sig_val", "idx", "is_dma", "hard")

    def __init__(self, eng, fn, reads, writes, dsem):
        self.eng = eng
        self.fn = fn
        self.reads = reads
        self.writes = writes
        self.dsem = dsem
        self.deps = []
        self.hard = set()
        self.signals = False
        self.sig_val = None
        self.is_dma = dsem is not None


class Sched:
    ENGS = ("pe", "act", "dve", "pool", "sp")

    def __init__(self, nc):
        self.nc = nc
        self.ops = []
        self.last_w = {}
        self.readers = {}
        self.last_eng = {}
        self.last_dma = {}
        self.pending = {}

    def add(self, eng, fn, reads=(), writes=(), dsem=None):
        op = Op(eng, fn, tuple(reads), tuple(writes), dsem)
        op.idx = len(self.ops)
        deps = set()
        for r in op.reads:
            w = self.last_w.get(r)
            if w is not None:
                deps.add(w)
        for r in op.writes:
            w = self.last_w.get(r)
            if w is not None:
                deps.add(w)
            for rd in self.readers.get(r, {}).values():
                deps.update(rd)
        if eng in self.pending:
            op.hard = self.pending.pop(eng)
            deps |= op.hard
        deps.discard(op.idx)
        op.deps = sorted(deps)
        for r in op.reads:
            rd = self.readers.setdefault(r, {})
            if op.is_dma:
                rd.setdefault("dma", []).append(op.idx)
            else:
                rd[eng] = [op.idx]
        for r in op.writes:
            self.last_w[r] = op.idx
            self.readers[r] = {}
        if op.is_dma:
            self.last_dma[dsem] = op.idx
        else:
            self.last_eng[eng] = op.idx
        self.ops.append(op)
        return op

    def load(self, eng, fn, dst, srcs=()):
        return self.add(eng, fn, reads=srcs, writes=[dst], dsem="L:" + str(dst))

    def store(self, eng, fn, src, dsts=()):
        return self.add(eng, fn, reads=[src], writes=dsts, dsem="S:" + str(src))

    def barrier(self):
        deps = set(self.last_eng.values()) | set(self.last_dma.values())
        for e in self.ENGS:
            self.pending[e] = set(deps) | self.pending.get(e, set())

    def emit(self, stack):
        nc = self.nc
        ops = self.ops
        need = []
        for op in ops:
            nd = []
            for d in op.deps:
                dop = ops[d]
                if dop.eng == op.eng and not dop.is_dma and not op.is_dma:
                    if op.eng == "pe":
                        continue
                    if d not in op.hard and not (set(dop.writes) & set(op.reads)):
                        continue
                nd.append(d)
                dop.signals = True
            need.append(nd)
        esem = {e: stack.enter_context(nc.semaphore("s_" + e)) for e in self.ENGS}
        dsems = {}
        cnt = {e: 0 for e in self.ENGS}
        dcnt = {}
        for op in ops:
            if op.is_dma:
                if op.dsem not in dsems:
                    dsems[op.dsem] = stack.enter_context(nc.semaphore("d%d" % len(dsems)))
                    dcnt[op.dsem] = 0
                dcnt[op.dsem] += 16
                op.sig_val = (dsems[op.dsem], dcnt[op.dsem])
                op.signals = True
            elif op.signals:
                cnt[op.eng] += 1
                op.sig_val = (esem[op.eng], cnt[op.eng])
        per_eng = {e: [] for e in self.ENGS}
        for op in ops:
            per_eng[op.eng].append(op)
        block = stack.enter_context(nc.Block())
        final = {k: (dsems[k], dcnt[k]) for k in dsems}
        self.stats = dict(n_ops=len(ops), sig=dict(cnt), n_dsem=len(dsems), dmax=max(dcnt.values()) if dcnt else 0,
                          per_eng={e: len(per_eng[e]) for e in per_eng})

        def body(ename):
            def run(eng):
                waited = {}
                for op in per_eng[ename]:
                    for d in need[op.idx]:
                        sem, val = ops[d].sig_val
                        key = id(sem)
                        if waited.get(key, 0) >= val:
                            continue
                        waited[key] = val
                        eng.wait_ge(sem, val)
                    ins = op.fn(eng)
                    if op.signals:
                        ins.then_inc(op.sig_val[0], 16 if op.is_dma else 1)
                if ename == "sp":
                    for k, (sem, val) in final.items():
                        if waited.get(id(sem), 0) < val:
                            eng.wait_ge(sem, val)
            return run

        block.tensor(body("pe"))
        block.scalar(body("act"))
        block.vector(body("dve"))
        block.gpsimd(body("pool"))
        block.sync(body("sp"))


class Arena:
    def __init__(self, tile, size):
        self.t = tile
        self.size = size
        self.off = 0

    def reset(self):
        self.off = 0

    def alloc(self, shape, dt=None):
        n = int(np.prod(shape))
        assert self.off + n <= self.size, ("arena overflow", self.off, n, self.size)
        ap = self.t[:, self.off:self.off + n]
        self.off += n
        if len(shape) == 2:
            return ap.rearrange("p (a b) -> p a b", b=shape[1])
        if len(shape) == 3:
            return ap.rearrange("p (a b c) -> p a b c", b=shape[1], c=shape[2])
        return ap


def bc(ap, shape):
    return ap.broadcast_to(list(shape))


def build(n_layers=NL, phases="ACEF", debug=False):
    nc = bass.Bass("TRN2", target_bir_lowering=False)

    def din(name, shape):
        return nc.dram_tensor(name, list(shape), F32, kind="ExternalInput").ap()

    def dout(name, shape):
        return nc.dram_tensor(name, list(shape), F32, kind="ExternalOutput").ap()

    def dscr(name, shape):
        return nc.dram_tensor(name, list(shape), F32, kind=("ExternalOutput" if debug else "Internal")).ap()

    x_all = din("x_all", [NTOK, D])
    c_fm = din("c_fm", [128, 8, 2])
    ck = din("ck", [NL, PAST, 256])
    cv = din("cv", [NL, PAST, 256])
    s0 = din("s0", [NL, 2, 2048, 128])
    ada_w = din("ada_w", [NL, D, 6 * D])
    w_in = din("w_in", [NL, D, IN_DIM])
    w_attn_o = din("w_attn_o", [NL, D, D])
    w_conv_o = din("w_conv_o", [NL, D, D])
    w_ssd_o = din("w_ssd_o", [NL, 2 * D, D])
    w_merge = din("w_merge", [NL, D, D])
    ffn_w1 = din("ffn_w1", [NL, D, 2 * DFF])
    ffn_w2 = din("ffn_w2", [NL, DFF, D])
    SM_W = 48 + 8 + 8 + 24 + 96 + 32 + 128 + 128 + 64 + 64 + 32
    small = din("small", [NL, 128, SM_W])
    ngrow_d = din("ngrow", [NL, 128, 2048])
    consts = din("consts", [128, 6 * 128])
    rope = din("rope", [128, 2, 16, 64])

    y_all = dout("y_all", [NTOK, D])
    new_k = dout("new_k", [4, NL, TP, 256])
    new_v = dout("new_v", [4, NL, TP, 256])
    new_ssd = dout("new_ssd", [4, NL, 2, 2048, 128])

    XST = dscr("XST", [D, NTOK])
    PTOK = dscr("PTOK", [NTOK, PT_W])
    PFM = dscr("PFM", [FM_ROWS, NTOK])
    XBCS = dscr("XBCS", [4096, NTOK])
    CVS = dscr("CVS", [D, NTOK])
    AOS = dscr("AOS", [D, NTOK])
    YF = dscr("YF", [NTOK, 2048])
    YST = dscr("YST", [2048, NTOK])

    with ExitStack() as st:
        ARENA_N = 15 * 1024
        arena_t = st.enter_context(nc.sbuf_tensor("arena", [128, ARENA_N], F32))
        AR = Arena(arena_t, ARENA_N)
        ARENAR_N = 28 * 1024
        arenar_t = st.enter_context(nc.sbuf_tensor("arenar", [128, ARENAR_N], F32R))
        ARr = Arena(arenar_t, ARENAR_N)
        PERS_N = 6200
        pers_t = st.enter_context(nc.sbuf_tensor("pers", [128, PERS_N], F32))
        PR = Arena(pers_t, PERS_N)
        persr_t = st.enter_context(nc.sbuf_tensor("persr", [128, 400], F32R))
        PRr = Arena(persr_t, 400)
        ps = [st.enter_context(nc.psum_tensor("ps%d" % i, [128, 512], F32)) for i in range(8)]
        S = Sched(nc)

        cst = PR.alloc([6, 128])
        ident, Mle, Mgt, Mlt, Mge, ones = [cst[:, i, :] for i in range(6)]
        cstr = PRr.alloc([3, 128])
        Mgt_r, Mlt_r, ones_r = [cstr[:, i, :] for i in range(3)]
        ropet = PR.alloc([2, 16, 64])
        csil = PRr.alloc([8, 2])
        craw = PR.alloc([8, 2])
        sm = PR.alloc([SM_W])
        o = 0
        adab = sm[:, o:o + 48]; o += 48
        n1g = sm[:, o:o + 8]; o += 8
        n2g = sm[:, o:o + 8]; o += 8
        convw = sm[:, o:o + 24].rearrange("p (a b) -> p a b", b=3); o += 24
        sconvw = sm[:, o:o + 96].rearrange("p (a b) -> p a b", b=3); o += 96
        sconvb = sm[:, o:o + 32]; o += 32
        qgrow = sm[:, o:o + 128]; o += 128
        kgrow = sm[:, o:o + 128]; o += 128
        dtbrow = sm[:, o:o + 64]; o += 64
        alogrow = sm[:, o:o + 64]; o += 64
        drow = sm[:, o:o + 32]; o += 32
        arow = PR.alloc([64])
        nconvw = PR.alloc([8, 3]); nsconvw = PR.alloc([32, 3])
        modv = PR.alloc([48, 2])
        A1 = PR.alloc([8, 2]); A2 = PR.alloc([8, 2])
        ngrow = PR.alloc([2048])
        B1 = modv[:, 0:8, :]; G1 = modv[:, 16:24, :]; B2 = modv[:, 24:32, :]; G2 = modv[:, 40:48, :]

        psk = ["ps%d" % i for i in range(8)]

        S.load("sp", lambda e: e.dma_start(out=cst, in_=consts.rearrange("p (a b) -> p a b", b=128)), "cst")
        S.load("sp", lambda e: e.dma_start(out=ropet, in_=rope), "rope")
        S.load("sp", lambda e: e.dma_start(out=craw, in_=c_fm), "craw")
        S.add("act", lambda e: e.activation(out=csil, in_=craw, func=AF.Silu), reads=["craw"], writes=["csil"])
        S.add("act", lambda e: e.activation(out=cstr[:, 0:2, :], in_=cst[:, 2:4, :], func=AF.Copy), reads=["cst"], writes=["cstr"])
        S.add("act", lambda e: e.activation(out=cstr[:, 2, :], in_=cst[:, 5, :], func=AF.Copy), reads=["cst"], writes=["cstr"])

        rr = {"ps": 0, "sg": 0, "wb": 0}

        def evac(eng, out, in_, reads, writes, func=None, scale=None, bias=None):
            if eng == "act":
                kw = {}
                if scale is not None:
                    kw["scale"] = scale
                if bias is not None:
                    kw["bias"] = bias
                S.add("act", lambda e: e.activation(out=out, in_=in_, func=(func or AF.Copy), **kw), reads=reads, writes=writes)
            else:
                assert func is None
                if scale is not None:
                    S.add("dve", lambda e: e.tensor_scalar(out=out, in0=in_, scalar1=scale, scalar2=bias, op0=ALU.mult, op1=ALU.add), reads=reads, writes=writes)
                else:
                    S.add("dve", lambda e: e.tensor_copy(out=out, in_=in_), reads=reads, writes=writes)

        def mm(out, lhsT, rhs, start, stop, reads, writes):
            S.add("pe", lambda e: e.matmul(out, lhsT=lhsT, rhs=rhs, start=start, stop=stop), reads=reads, writes=writes)

        def tr(out, in_, reads, writes):
            S.add("pe", lambda e: e.transpose(out=out, in_=in_, identity=ident), reads=list(reads) + ["cst"], writes=writes)

        def wload(dst, src, key):
            S.load("pool", lambda e: e.dma_start(out=dst, in_=src), key)

        def layer_setup(l):
            S.barrier()
            AR.reset(); ARr.reset()
            S.load("sp", lambda e: e.dma_start(out=sm, in_=small[l]), "sm")
            S.load("sp", lambda e: e.dma_start(out=ngrow, in_=ngrow_d[l]), "ngrow")
            S.add("dve", lambda e: e.tensor_scalar(out=nconvw, in0=convw, scalar1=-1.0, scalar2=None, op0=ALU.mult), reads=["sm"], writes=["nconvw"])
            S.add("dve", lambda e: e.tensor_scalar(out=nsconvw, in0=sconvw, scalar1=-1.0, scalar2=None, op0=ALU.mult), reads=["sm"], writes=["nconvw"])
            S.add("act", lambda e: e.activation(out=arow, in_=alogrow, func=AF.Exp), reads=["sm"], writes=["arow"])
            S.add("dve", lambda e: e.tensor_scalar(out=arow, in0=arow, scalar1=-1.0, scalar2=None, op0=ALU.mult), reads=["arow"], writes=["arow"])
            wb = [ARr.alloc([8, 512]) for _ in range(2)]
            pm = ps[0][:, 0:96].rearrange("p (a b) -> p a b", b=2)
            for blk in range(12):
                s = blk % 2
                wload(wb[s], ada_w[l][:, blk * 512:(blk + 1) * 512].rearrange("(kc p) n -> p kc n", p=128), "wb%d" % s)
                for fo in range(4):
                    fc = blk * 4 + fo
                    for kc in range(8):
                        mm(pm[:, fc, :], wb[s][:, kc, fo * 128:(fo + 1) * 128], csil[:, kc, :], (blk == 0 and fo == 0 and kc == 0), kc == 7,
                           ["wb%d" % s, "csil"], ["ps0"])
            S.add("dve", lambda e: e.tensor_tensor(out=modv, in0=pm, in1=bc(adab.unsqueeze(2), [128, 48, 2]), op=ALU.add), reads=["ps0", "sm"], writes=["modv"])
            for (Ax, ng, lo) in ((A1, n1g, 8), (A2, n2g, 32)):
                S.add("dve", lambda e, Ax=Ax, lo=lo: e.tensor_scalar(out=Ax, in0=modv[:, lo:lo + 8, :], scalar1=1.0, scalar2=None, op0=ALU.add), reads=["modv"], writes=["A12"])
                S.add("dve", lambda e, Ax=Ax, ng=ng: e.tensor_tensor(out=Ax, in0=Ax, in1=bc(ng.unsqueeze(2), [128, 8, 2]), op=ALU.mult), reads=["A12", "sm"], writes=["A12"])

        def ln_fm(xT, hT, sq, n, A, B, kind, pbank, keys_in, key_out, rstd, xn, sqkey="sq"):
            S.add("act", lambda e: e.activation(out=sq[:, :, 0:n], in_=xT[:, :, 0:n], func=AF.Square), reads=keys_in, writes=[sqkey])
            for kc in range(8):
                mm(ps[pbank][:, 0:n], ones_r, sq[:, kc, 0:n], kc == 0, kc == 7, [sqkey, "cstr"], [psk[pbank]])
            S.add("act", lambda e: e.activation(out=rstd[:, 0:n], in_=ps[pbank][:, 0:n], func=AF.Ln, scale=1.0 / D, bias=EPS), reads=[psk[pbank]], writes=["rstd"])
            S.add("act", lambda e: e.activation(out=rstd[:, 0:n], in_=rstd[:, 0:n], func=AF.Exp, scale=-0.5), reads=["rstd"], writes=["rstd"])
            for kc in range(8):
                eng = "dve" if kc % 2 == 0 else "pool"
                S.add(eng, lambda e, kc=kc: e.tensor_tensor(out=xn[kc % 2][:, 0:n], in0=xT[:, kc, 0:n], in1=rstd[:, 0:n], op=ALU.mult),
                      reads=list(keys_in) + ["rstd"], writes=["xn%d" % (kc % 2)])
                evac("act" if kc % 2 == 0 else "dve", hT[:, kc, 0:n], xn[kc % 2][:, 0:n], ["xn%d" % (kc % 2), "A12", "modv"], [key_out],
                     func=(AF.Identity if kc % 2 == 0 else None), scale=A[:, kc, kind:kind + 1], bias=B[:, kc, kind:kind + 1])

        def load_xT(l, t0, n, xT, xtok, key):
            if l == 0:
                for i in range(n // 128):
                    S.load("sp", lambda e, i=i: e.dma_start(out=xtok[i % 2], in_=x_all[t0 + i * 128:t0 + (i + 1) * 128, :]), "xtok%d" % (i % 2))
                    for h in range(2):
                        b = 6 + h
                        for j in range(4):
                            kc = h * 4 + j
                            tr(ps[b][:, j * 128:(j + 1) * 128], xtok[i % 2][:, kc * 128:(kc + 1) * 128], ["xtok%d" % (i % 2)], [psk[b]])
                        evac("act" if h == 0 else "dve", xT[:, h * 4:(h + 1) * 4, i * 128:(i + 1) * 128],
                             ps[b][:, :].rearrange("p (a b) -> p a b", b=128), [psk[b]], [key])
            else:
                S.load("sp", lambda e: e.dma_start(out=xT[:, :, 0:n], in_=XST[:, t0:t0 + n].rearrange("(kc p) n -> p kc n", p=128)), key, ["XST"])

        def conv3(x, acc, n, nseg, w, nw, rk, wk):
            w0, w1, w2 = w[:, 0:1], w[:, 1:2], w[:, 2:3]
            S.add("dve", lambda e: e.tensor_scalar(out=acc[:, 0:n], in0=x[:, 0:n], scalar1=w1, scalar2=None, op0=ALU.mult), reads=[rk, "sm"], writes=[wk])
            S.add("dve", lambda e: e.scalar_tensor_tensor(out=acc[:, 1:n], in0=x[:, 0:n - 1], scalar=w0, in1=acc[:, 1:n], op0=ALU.mult, op1=ALU.add), reads=[rk, "sm", wk], writes=[wk])
            S.add("dve", lambda e: e.scalar_tensor_tensor(out=acc[:, 0:n - 1], in0=x[:, 1:n], scalar=w2, in1=acc[:, 0:n - 1], op0=ALU.mult, op1=ALU.add), reads=[rk, "sm", wk], writes=[wk])
            if nseg > 1:
                T = n // nseg
                av = acc[:, 0:n].rearrange("p (s t) -> p s t", t=T)
                xv = x[:, 0:n].rearrange("p (s t) -> p s t", t=T)
                S.add("dve", lambda e: e.scalar_tensor_tensor(out=av[:, 1:nseg, 0], in0=xv[:, 0:nseg - 1, T - 1], scalar=nw[:, 0:1], in1=av[:, 1:nseg, 0], op0=ALU.mult, op1=ALU.add),
                      reads=[rk, "nconvw", wk], writes=[wk])
                S.add("dve", lambda e: e.scalar_tensor_tensor(out=av[:, 0:nseg - 1, T - 1], in0=xv[:, 1:nseg, 0], scalar=nw[:, 2:3], in1=av[:, 0:nseg - 1, T - 1], op0=ALU.mult, op1=ALU.add),
                      reads=[rk, "nconvw", wk], writes=[wk])

        def phaseAB(l, g):
            S.barrier()
            AR.reset(); ARr.reset()
            g0, gn, _ = GROUPS[g]
            kind = g
            HT = ARr.alloc([8, 2048])
            wb = [ARr.alloc([8, 512]) for _ in range(2)]
            xT = AR.alloc([8, 512])
            sq = ARr.alloc([8, 512])
            rstd = AR.alloc([512])
            xtok = [AR.alloc([1024]) for _ in range(2)]
            xn = [AR.alloc([512]) for _ in range(2)]
            nt = gn // 512

            def conv_wloads(kc, s):
                for j, c0 in enumerate((C_CX, C_CG, C_BG)):
                    wload(wb[s][:, :, j * 128:(j + 1) * 128], w_in[l][:, c0 + kc * 128:c0 + (kc + 1) * 128].rearrange("(kc p) n -> p kc n", p=128), "wb%d" % s)

            conv_wloads(0, 0); conv_wloads(1, 1)
            for t in range(nt):
                load_xT(l, g0 + t * 512, 512, xT, xtok, "xT")
                ln_fm(xT, HT[:, :, t * 512:(t + 1) * 512], sq, 512, A1, B1, kind, 5, ["xT"], "HT%d" % t, rstd, xn)
            S.barrier()
            AR.reset()
            sg = [AR.alloc([512]) for _ in range(4)]
            G = [AR.alloc([2048]) for _ in range(4)]
            acc = [AR.alloc([2048]) for _ in range(2)]
            segs = [(SEQS[si][0] - g0, SEQS[si][1]) for si in GROUPS[g][2]]
            wc = [0]

            def proj_chunk(wv, dst, dkey, wkey):
                for t in range(nt):
                    b = rr["ps"] % 4; rr["ps"] += 1
                    for kc in range(8):
                        mm(ps[b][:, :], wv[:, kc, :], HT[:, kc, t * 512:(t + 1) * 512], kc == 0, kc == 7, ["HT%d" % t, wkey], [psk[b]])
                    evac("act", dst[:, t * 512:(t + 1) * 512], ps[b][:, :], [psk[b]], [dkey])

            nseg = len(segs)

            def proj_mul(wv, dst, dkey, mul, mkey, wkey):
                for t in range(nt):
                    b = rr["ps"] % 4; rr["ps"] += 1
                    tsl = slice(t * 512, (t + 1) * 512)
                    for kc in range(8):
                        mm(ps[b][:, :], wv[:, kc, :], HT[:, kc, tsl], kc == 0, kc == 7, ["HT%d" % t, wkey], [psk[b]])
                    S.add("dve", lambda e, b=b, tsl=tsl: e.tensor_tensor(out=dst[:, tsl], in0=ps[b][:, :], in1=mul[:, tsl], op=ALU.mult), reads=[psk[b], mkey], writes=[dkey])

            for kc in range(8):
                s = wc[0] % 2; wc[0] += 1
                if kc >= 2:
                    conv_wloads(kc, s)
                Gcx, cxk = G[2 * (kc % 2)], "G%d" % (2 * (kc % 2))
                Gu, uk = G[2 * (kc % 2) + 1], "G%d" % (2 * (kc % 2) + 1)
                a_, ak = acc[kc % 2], "acc%d" % (kc % 2)
                proj_chunk(wb[s][:, :, 0:128], Gcx, cxk, "wb%d" % s)
                proj_mul(wb[s][:, :, 128:256], Gu, uk, Gcx, cxk, "wb%d" % s)
                conv3(Gu, a_, gn, nseg, convw[:, kc, :], nconvw[:, kc, :], uk, ak)
                proj_mul(wb[s][:, :, 256:384], a_, ak, a_, ak, "wb%d" % s)
                S.store("sp", lambda e, a_=a_, kc=kc: e.dma_start(out=CVS[kc * 128:(kc + 1) * 128, g0:g0 + gn], in_=a_[:, 0:gn]), ak, ["CVS"])
            for blk in range(8):
                s = wc[0] % 2; wc[0] += 1
                wload(wb[s], w_in[l][:, C_XBC + blk * 512:C_XBC + (blk + 1) * 512].rearrange("(kc p) n -> p kc n", p=128), "wb%d" % s)
                for fo in range(4):
                    fc = blk * 4 + fo
                    gi = fc % 4
                    a_, ak = acc[fc % 2], "acc%d" % (fc % 2)
                    proj_chunk(wb[s][:, :, fo * 128:(fo + 1) * 128], G[gi], "G%d" % gi, "wb%d" % s)
                    conv3(G[gi], a_, gn, nseg, sconvw[:, fc, :], nsconvw[:, fc, :], "G%d" % gi, ak)
                    S.add("act", lambda e, a_=a_, fc=fc: e.activation(out=a_[:, 0:gn], in_=a_[:, 0:gn], func=AF.Silu, bias=sconvb[:, fc:fc + 1]), reads=[ak, "sm"], writes=[ak])
                    S.store("sp", lambda e, a_=a_, fc=fc: e.dma_start(out=XBCS[fc * 128:(fc + 1) * 128, g0:g0 + gn], in_=a_[:, 0:gn]), ak, ["XBCS"])
            jobs = []
            for b in range(3):
                jobs.append((b * 512, 512, "tok", b * 512, None))
            for b in range(4):
                jobs.append((C_Z + b * 512, 512, "tok", 1536 + b * 512, AF.Silu))
            jobs.append((C_DT, 64, "tok", 3584, None))
            for b in range(6):
                jobs.append((C_GT + b * 512, 512, "fm", FM_GT + b * 512, AF.Sigmoid))
            for ji, (c0, ncol, mode, dest, func) in enumerate(jobs):
                s = wc[0] % 2; wc[0] += 1
                wload(wb[s][:, :, 0:ncol], w_in[l][:, c0:c0 + ncol].rearrange("(kc p) n -> p kc n", p=128), "wb%d" % s)
                if mode == "tok":
                    for tt in range(gn // 128):
                        b = rr["ps"] % 4; rr["ps"] += 1
                        q = rr["sg"] % 4; rr["sg"] += 1
                        for kc in range(8):
                            lh = HT[:, kc, tt * 128:(tt + 1) * 128]
                            rh = wb[s][:, kc, 0:ncol]
                            if ncol < 256:
                                lh = lh.bitcast(F32); rh = rh.bitcast(F32)
                            mm(ps[b][:, 0:ncol], lh, rh, kc == 0, kc == 7, ["HT%d" % (tt // 4), "wb%d" % s], [psk[b]])
                        use_act = (func is not None) or (tt % 2 == 0)
                        evac("act" if use_act else "dve", sg[q][:, 0:ncol], ps[b][:, 0:ncol], [psk[b]], ["sg%d" % q], func=func)
                        S.store("sp", lambda e, q=q, tt=tt, dest=dest, ncol=ncol: e.dma_start(
                            out=PTOK[g0 + tt * 128:g0 + (tt + 1) * 128, dest:dest + ncol], in_=sg[q][:, 0:ncol]), "sg%d" % q, ["PTOK"])
                else:
                    for fo in range(4):
                        for t in range(nt):
                            b = rr["ps"] % 4; rr["ps"] += 1
                            q = rr["sg"] % 4; rr["sg"] += 1
                            for kc in range(8):
                                mm(ps[b][:, :], wb[s][:, kc, fo * 128:(fo + 1) * 128], HT[:, kc, t * 512:(t + 1) * 512], kc == 0, kc == 7,
                                   ["HT%d" % t, "wb%d" % s], [psk[b]])
                            use_act = (func is not None) or ((fo + t) % 2 == 0)
                            evac("act" if use_act else "dve", sg[q], ps[b][:, :], [psk[b]], ["sg%d" % q], func=func)
                            r0 = dest + fo * 128
                            S.store("sp", lambda e, q=q, r0=r0, t=t: e.dma_start(
                                out=PFM[r0:r0 + 128, g0 + t * 512:g0 + (t + 1) * 512], in_=sg[q]), "sg%d" % q, ["PFM"])


        def rope_ops(src, dst, tmp, H, i, rk, wk):
            cs = bc(ropet[:, 0, i, :].unsqueeze(1), [128, H, 64])
            sn = bc(ropet[:, 1, i, :].unsqueeze(1), [128, H, 64])
            sv = src.rearrange("p (h d two) -> p h d two", h=H, two=2)
            dv = dst.rearrange("p (h d two) -> p h d two", h=H, two=2)
            tv = tmp[:, 0:H * 64].rearrange("p (h d) -> p h d", h=H)
            xe, xo, de, do = sv[:, :, :, 0], sv[:, :, :, 1], dv[:, :, :, 0], dv[:, :, :, 1]
            S.add("dve", lambda e: e.tensor_tensor(out=de, in0=xe, in1=cs, op=ALU.mult), reads=[rk, "rope"], writes=[wk])
            S.add("pool", lambda e: e.tensor_tensor(out=tv, in0=xo, in1=sn, op=ALU.mult), reads=[rk, "rope"], writes=["ropetmp"])
            S.add("dve", lambda e: e.tensor_tensor(out=de, in0=de, in1=tv, op=ALU.subtract), reads=[wk, "ropetmp"], writes=[wk])
            S.add("dve", lambda e: e.tensor_tensor(out=do, in0=xe, in1=sn, op=ALU.mult), reads=[rk, "rope"], writes=[wk])
            S.add("pool", lambda e: e.tensor_tensor(out=tv, in0=xo, in1=cs, op=ALU.mult), reads=[rk, "rope", wk], writes=["ropetmp"])
            S.add("dve", lambda e: e.tensor_tensor(out=do, in0=do, in1=tv, op=ALU.add), reads=[wk, "ropetmp"], writes=[wk])

        def qk_norm(src, dst, H, grow, sqk, ssq, rk, wk, part="ab", sk="ssq"):
            n = H * 128
            if "a" in part:
                S.add("pool", lambda e: e.tensor_tensor(out=sqk[:, 0:n], in0=src, in1=src, op=ALU.mult), reads=[rk], writes=["sqk"])
                S.add("dve", lambda e: e.tensor_reduce(out=ssq[:, 0:H], in_=sqk[:, 0:n].rearrange("p (h d) -> p h d", d=128), op=ALU.add, axis=AX.X), reads=["sqk"], writes=[sk])
            if "b" in part:
                S.add("act", lambda e: e.activation(out=ssq[:, 0:H], in_=ssq[:, 0:H], func=AF.Ln, scale=1.0 / 128, bias=EPS), reads=[sk], writes=[sk])
                S.add("act", lambda e: e.activation(out=ssq[:, 0:H], in_=ssq[:, 0:H], func=AF.Exp, scale=-0.5), reads=[sk], writes=[sk])
                d3 = dst.rearrange("p (h d) -> p h d", d=128)
                S.add("dve", lambda e: e.tensor_tensor(out=d3, in0=src.rearrange("p (h d) -> p h d", d=128), in1=bc(ssq[:, 0:H].unsqueeze(2), [128, H, 128]), op=ALU.mult), reads=[rk, sk], writes=[wk])
                S.add("pool", lambda e: e.tensor_tensor(out=d3, in0=d3, in1=bc(grow.unsqueeze(1), [128, H, 128]), op=ALU.mult), reads=[wk, "sm"], writes=[wk])

        def phaseC(l, g):
            S.barrier()
            AR.reset(); ARr.reset()
            KT = ARr.alloc([2, 2560])
            VV = ARr.alloc([20, 256])
            tok = [AR.alloc([1536]) for _ in range(2)]
            sqk = AR.alloc([1024]); ssq = AR.alloc([8]); qn = AR.alloc([1024]); qr = AR.alloc([1024]); tmp = AR.alloc([512])
            qnb = [qn, AR.alloc([1024])]; qrb = [qr, AR.alloc([1024])]; ssqb = [AR.alloc([8]) for _ in range(2)]
            QT = [ARr.alloc([8, 128]) for _ in range(2)]
            PTt = [ARr.alloc([512]) for _ in range(4)]
            rden = [AR.alloc([512]) for _ in range(2)]; osb = [AR.alloc([512]) for _ in range(2)]
            cnt = 0

            def seq_body(si, t0, T, lat):
                nonlocal cnt
                ntile = T // 128
                nkt = ntile + (4 if lat else 0)
                for i in range(ntile):
                    s = cnt % 2; cnt += 1
                    tk = "tok%d" % s
                    S.load("sp", lambda e, s=s, i=i: e.dma_start(out=tok[s][:, 0:512], in_=PTOK[t0 + i * 128:t0 + (i + 1) * 128, 1024:1536]), tk, ["PTOK"])
                    qn_, qr_, pk = qnb[i % 2], qrb[i % 2], i % 2
                    qk_norm(tok[s][:, 0:256], qn_[:, 0:256], 2, kgrow, sqk, ssqb[pk], tk, "qn%d" % pk, sk="ssq%d" % pk)
                    ksrc, kk = qn_, "qn%d" % pk
                    if not lat:
                        S.store("sp", lambda e, i=i, si=si, qn_=qn_: e.dma_start(out=new_k[si, l, i * 128:(i + 1) * 128, :], in_=qn_[:, 0:256]), "qn%d" % pk)
                        S.store("sp", lambda e, i=i, si=si, s=s: e.dma_start(out=new_v[si, l, i * 128:(i + 1) * 128, :], in_=tok[s][:, 256:512]), tk)
                    else:
                        rope_ops(qn_[:, 0:256], qr_[:, 0:256], tmp, 2, i, "qn%d" % pk, "qr%d" % pk)
                        ksrc, kk = qr_, "qr%d" % pk
                    for kv in range(2):
                        tr(ps[6][:, kv * 128:(kv + 1) * 128], ksrc[:, kv * 128:(kv + 1) * 128], [kk], ["ps6"])
                    evac("dve", KT[:, :, i * 128:(i + 1) * 128], ps[6][:, 0:256].rearrange("p (a b) -> p a b", b=128), ["ps6"], ["KT"])
                    evac("dve", VV[:, i, :], tok[s][:, 256:512], [tk], ["VV"])
                if lat:
                    for j in range(4):
                        s = cnt % 2; cnt += 1
                        tk = "tok%d" % s
                        S.load("sp", lambda e, s=s, j=j: e.dma_start(out=tok[s][:, 0:256], in_=ck[l, j * 128:(j + 1) * 128, :]), tk)
                        S.load("sp", lambda e, s=s, j=j: e.dma_start(out=tok[s][:, 256:512], in_=cv[l, j * 128:(j + 1) * 128, :]), tk)
                        for kv in range(2):
                            tr(ps[6][:, kv * 128:(kv + 1) * 128], tok[s][:, kv * 128:(kv + 1) * 128], [tk], ["ps6"])
                        evac("dve", KT[:, :, T + j * 128:T + (j + 1) * 128], ps[6][:, 0:256].rearrange("p (a b) -> p a b", b=128), ["ps6"], ["KT"])
                        evac("dve", VV[:, ntile + j, :], tok[s][:, 256:512], [tk], ["VV"])
                qst = {}

                def q_pro_a(i):
                    nonlocal cnt
                    s = cnt % 2; cnt += 1
                    tk = "tok%d" % s
                    qst[i] = (s, tk)
                    S.load("sp", lambda e: e.dma_start(out=tok[s][:, 0:1024], in_=PTOK[t0 + i * 128:t0 + (i + 1) * 128, 0:1024]), tk, ["PTOK"])
                    qk_norm(tok[s][:, 0:1024], qnb[i % 2], 8, qgrow, sqk, ssqb[i % 2], tk, "qn%d" % (i % 2), part="a", sk="ssq%d" % (i % 2))

                def q_pro_b(i):
                    s, tk = qst[i]
                    qk_norm(tok[s][:, 0:1024], qnb[i % 2], 8, qgrow, sqk, ssqb[i % 2], tk, "qn%d" % (i % 2), part="b", sk="ssq%d" % (i % 2))
                    if lat:
                        rope_ops(qnb[i % 2], qrb[i % 2], tmp, 8, i, "qn%d" % (i % 2), "qr%d" % (i % 2))

                def q_pro_c(i):
                    qsrc, qk_ = (qrb[i % 2], "qr%d" % (i % 2)) if lat else (qnb[i % 2], "qn%d" % (i % 2))
                    for h2 in range(2):
                        b = 6 + h2
                        for j in range(4):
                            hh = h2 * 4 + j
                            tr(ps[b][:, j * 128:(j + 1) * 128], qsrc[:, hh * 128:(hh + 1) * 128], [qk_], [psk[b]])
                        evac("dve", QT[i % 2][:, h2 * 4:(h2 + 1) * 4, :], ps[b][:, :].rearrange("p (a b) -> p a b", b=128), [psk[b]], ["QT%d" % (i % 2)])

                q_pro_a(0); q_pro_b(0); q_pro_c(0)
                for i in range(ntile):
                    qs = i % 2
                    steps = [(kvg, kt) for kvg in range(2) for kt in range(nkt)]
                    LA = 3
                    SB = (0, 1, 6, 7)

                    def st_mm(n):
                        kvg, kt = steps[n]
                        b = SB[n % 4]
                        mm(ps[b][:, :], KT[:, kvg, kt * 128:(kt + 1) * 128], QT[qs][:, kvg * 4:(kvg + 1) * 4, :].rearrange("p a b -> p (a b)"),
                           True, True, ["KT", "QT%d" % qs], [psk[b]])

                    for n in range(min(LA, len(steps))):
                        st_mm(n)
                    for n, (kvg, kt) in enumerate(steps):
                        b = n % 4
                        sbk = SB[b]
                        oa, dn = 2 + 2 * kvg, 3 + 2 * kvg
                        S.add("act", lambda e, b=b, sbk=sbk: e.activation(out=PTt[b], in_=ps[sbk][:, :], func=AF.Exp, scale=float(128 ** -0.5)), reads=[psk[sbk]], writes=["PT%d" % b])
                        if n + LA < len(steps):
                            st_mm(n + LA)
                        if i + 1 < ntile and n == 1:
                            q_pro_a(i + 1)
                        if i + 1 < ntile and n == min(14, len(steps) - 1):
                            q_pro_b(i + 1)
                        mm(ps[oa][:, :], VV[:, kt, kvg * 128:(kvg + 1) * 128], PTt[b], kt == 0, kt == nkt - 1, ["VV", "PT%d" % b], [psk[oa]])
                        mm(ps[dn][:, :], ones_r, PTt[b], kt == 0, kt == nkt - 1, ["cstr", "PT%d" % b], [psk[dn]])
                        if kt == nkt - 1:
                            ob = kvg
                            S.add("dve", lambda e, ob=ob, dn=dn: e.reciprocal(out=rden[ob], in_=ps[dn][:, :]), reads=[psk[dn]], writes=["rden%d" % ob])
                            S.add("dve", lambda e, ob=ob, oa=oa: e.tensor_tensor(out=osb[ob], in0=ps[oa][:, :], in1=rden[ob], op=ALU.mult), reads=[psk[oa], "rden%d" % ob], writes=["osb%d" % ob])
                            S.store("sp", lambda e, ob=ob, kvg=kvg, i=i: e.dma_start(
                                out=AOS[kvg * 512:(kvg + 1) * 512, t0 + i * 128:t0 + (i + 1) * 128].rearrange("(h p) n -> p h n", p=128),
                                in_=osb[ob].rearrange("p (h n) -> p h n", n=128)), "osb%d" % ob, ["AOS"])
                    if i + 1 < ntile:
                        q_pro_c(i + 1)

            for si in GROUPS[g][2]:
                seq_body(si, *SEQS[si])

        def phaseE(l, g):
            S.barrier()
            AR.reset(); ARr.reset()
            XB = [ARr.alloc([32, 128]) for _ in range(2)]
            Btok = [ARr.alloc([1024]) for _ in range(2)]; xd = [ARr.alloc([2048]) for _ in range(2)]; xdw = [ARr.alloc([2048]) for _ in range(2)]
            Rt = [ARr.alloc([16, 128]) for _ in range(2)]
            Sst2 = [ARr.alloc([2048]) for _ in range(2)]
            dcount = [0]
            xs_tok = AR.alloc([2048]); dtt = [AR.alloc([64]) for _ in range(2)]; dat = [AR.alloc([64]) for _ in range(2)]; dec = [AR.alloc([96]) for _ in range(2)]
            cbm = [AR.alloc([4, 128]) for _ in range(2)]
            Et = [AR.alloc([512]) for _ in range(4)]; Mt = [AR.alloc([512]) for _ in range(4)]
            ych = AR.alloc([2048]); ytmp = [AR.alloc([256]) for _ in range(2)]; zt = AR.alloc([2048]); yft = AR.alloc([2048]); ss1 = AR.alloc([2])
            cnt = 0

            def dir_body(si, t0, T, lat, d):
                    nonlocal cnt
                    nch = T // 128
                    sbi = dcount[0] % 2; dcount[0] += 1
                    Sst = Sst2[sbi]
                    SK = "Sst%d_" % sbi
                    Lend, Lacs = (Mgt, Mle) if d == 0 else (Mlt, Mge)
                    Lseg_r = Mgt_r if d == 0 else Mlt_r
                    tri = Mle if d == 0 else Mge
                    if lat:
                        S.load("sp", lambda e, d=d: e.dma_start(out=yft.rearrange("p (j n) -> p j n", n=128), in_=s0[l, d].rearrange("(j p) n -> p j n", p=128)), "yft")
                        for q in range(4):
                            b = 5 + q % 2
                            for j in range(4):
                                tr(ps[b][:, j * 128:(j + 1) * 128], yft[:, (q * 4 + j) * 128:(q * 4 + j + 1) * 128], ["yft"], [psk[b]])
                            evac("act" if q % 2 == 0 else "dve", Sst[:, q * 512:(q + 1) * 512], ps[b][:, :], [psk[b]], [SK + str(2 * q), SK + str(2 * q + 1)])
                    else:
                        S.add("dve", lambda e: e.tensor_scalar(out=Sst, in0=ngrow, scalar1=0.0, scalar2=None, op0=ALU.mult), reads=["ngrow"], writes=[SK + str(k) for k in range(8)])
                    cnt0 = cnt
                    cnt += nch

                    def issue_loads(cc):
                        c = cc if d == 0 else nch - 1 - cc
                        tc0 = t0 + c * 128
                        s = (cnt0 + cc) % 2
                        S.load("pool", lambda e: e.dma_start(out=XB[s], in_=XBCS[:, tc0:tc0 + 128].rearrange("(fc p) n -> p fc n", p=128)), "XB%d" % s, ["XBCS"])
                        S.load("sp", lambda e: e.dma_start(out=dtt[s], in_=PTOK[tc0:tc0 + 128, 3584:3648]), "dtt%d" % s, ["PTOK"])

                    issue_loads(0)

                    def chunk_body(cc):
                        c = cc if d == 0 else nch - 1 - cc
                        tc0 = t0 + c * 128
                        s = (cnt0 + cc) % 2
                        xk = "XB%d" % s
                        XBc = XB[s]
                        dtt_, dat_, dec_ = dtt[s], dat[s], dec[s]
                        Btok_, xd_, xdw_ = Btok[s], xd[s], xdw[s]
                        kd, ka, ke, kb, kx, kw = "dtt%d" % s, "dat%d" % s, "dec%d" % s, "Btok%d" % s, "xd%d" % s, "xdw%d" % s
                        if d == 1:
                            S.load("sp", lambda e, tc0=tc0: e.dma_start(out=yft, in_=YF[tc0:tc0 + 128, :]), "yft", ["YF"])
                            S.load("sp", lambda e, tc0=tc0: e.dma_start(out=zt, in_=PTOK[tc0:tc0 + 128, 1536:3584]), "zt", ["PTOK"])
                        S.add("dve", lambda e: e.tensor_tensor(out=dtt_, in0=dtt_, in1=dtbrow, op=ALU.add), reads=[kd, "sm"], writes=[kd])
                        S.add("act", lambda e: e.activation(out=dtt_, in_=dtt_, func=AF.Exp), reads=[kd], writes=[kd])
                        S.add("act", lambda e: e.activation(out=dtt_, in_=dtt_, func=AF.Ln, bias=1.0), reads=[kd], writes=[kd])
                        S.add("dve", lambda e: e.tensor_tensor(out=dat_, in0=dtt_, in1=arow, op=ALU.mult), reads=[kd, "arow"], writes=[ka])
                        for half in range(2):
                            S.add("dve" if half == 0 else "pool", lambda e, half=half: e.tensor_tensor(out=Rt[half], in0=bc(tri.unsqueeze(1), [128, 16, 128]),
                                  in1=bc(dat_[:, d * 32 + half * 16:d * 32 + half * 16 + 16].unsqueeze(2), [128, 16, 128]), op=ALU.mult), reads=["cst", ka], writes=["Rt%d" % half])
                        for q in range(4):
                            b = 5 + q % 2
                            for j in range(4):
                                tr(ps[b][:, j * 128:(j + 1) * 128], XBc[:, q * 4 + j, :].bitcast(F32), [xk], [psk[b]])
                            evac("act", xs_tok[:, q * 512:(q + 1) * 512], ps[b][:, :], [psk[b]], ["xs_tok"])
                        for q in range(2):
                            b = 5 + q % 2
                            for j in range(4):
                                tr(ps[b][:, j * 128:(j + 1) * 128], XBc[:, 16 + q * 4 + j, :].bitcast(F32), [xk], [psk[b]])
                            evac("act", Btok_[:, q * 512:(q + 1) * 512], ps[b][:, :], [psk[b]], [kb])
                        dsl = dat_[:, d * 32:(d + 1) * 32]
                        mm(ps[7][:, 0:32], Lend, dsl, True, True, ["cst", ka], ["ps7"])
                        mm(ps[7][:, 32:64], Lacs, dsl, False, True, ["cst", ka], ["ps7"])
                        mm(ps[7][:, 64:96], ones, dsl, False, True, ["cst", ka], ["ps7"])
                        S.add("act", lambda e: e.activation(out=dec_, in_=ps[7][:, 0:96], func=AF.Exp), reads=["ps7"], writes=[ke])
                        v3 = lambda a, h=32: a.rearrange("p (h q) -> p h q", h=h)
                        S.add("dve", lambda e: e.tensor_tensor(out=v3(xd_), in0=v3(xs_tok), in1=bc(dtt_[:, d * 32:(d + 1) * 32].unsqueeze(2), [128, 32, 64]), op=ALU.mult),
                              reads=["xs_tok", kd], writes=[kx])
                        S.add("pool", lambda e: e.tensor_tensor(out=v3(xdw_), in0=v3(xd_), in1=bc(dec_[:, 0:32].unsqueeze(2), [128, 32, 64]), op=ALU.mult),
                              reads=[kx, ke], writes=[kw])
                        if d == 0:
                            S.add("pool", lambda e: e.tensor_tensor(out=v3(yft), in0=v3(xs_tok), in1=bc(drow.unsqueeze(2), [128, 32, 64]), op=ALU.mult),
                                  reads=["xs_tok", "sm"], writes=["yft"])
                        if cc + 1 < nch:
                            issue_loads(cc + 1)
                        for half in range(2):
                            gs = [half * 4 + k for k in range(4)]
                            cb_, kc_ = cbm[half], "cbm%d" % half
                            for k, gi in enumerate(gs):
                                mm(ps[4][:, k * 128:(k + 1) * 128], XBc[:, 16 + gi, :].bitcast(F32), XBc[:, 24 + gi, :].bitcast(F32), k == 0, True, [xk], ["ps4"])
                            S.add("dve", lambda e, cb_=cb_: e.tensor_tensor(out=cb_, in0=ps[4][:, :].rearrange("p (k i) -> p k i", k=4), in1=bc(tri.unsqueeze(1), [128, 4, 128]), op=ALU.mult),
                                  reads=["ps4", "cst"], writes=[kc_])
                            for k, gi in enumerate(gs):
                                mm(ps[k][:, :], Lseg_r, Rt[half][:, k * 4:(k + 1) * 4, :].rearrange("p a b -> p (a b)"), True, True, ["cstr", "Rt%d" % half], [psk[k]])
                            for k, gi in enumerate(gs):
                                S.add("act", lambda e, k=k: e.activation(out=Et[k], in_=ps[k][:, :], func=AF.Exp), reads=[psk[k]], writes=["Et%d" % k])
                            for k, gi in enumerate(gs):
                                S.add("dve" if k % 2 == 0 else "pool", lambda e, k=k, cb_=cb_: e.tensor_tensor(out=Mt[k].rearrange("p (h i) -> p h i", h=4), in0=Et[k].rearrange("p (h i) -> p h i", h=4),
                                      in1=bc(cb_[:, k, :].unsqueeze(1), [128, 4, 128]), op=ALU.mult), reads=["Et%d" % k, kc_], writes=["Mt%d" % k])
                            for k, gi in enumerate(gs):
                                for h in range(4):
                                    hh = gi * 4 + h
                                    mm(ps[k][:, h * 64:(h + 1) * 64], Mt[k][:, h * 128:(h + 1) * 128], xd_[:, hh * 64:(hh + 1) * 64].bitcast(F32), h == 0, True,
                                       ["Mt%d" % k, kx], [psk[k]])
                                mm(ps[k][:, 256:512], XBc[:, 24 + gi, :], Sst[:, gi * 256:(gi + 1) * 256], False, True, [xk, SK + str(gi)], [psk[k]])
                            for k, gi in enumerate(gs):
                                yt_, ky = ytmp[k % 2], "ytmp%d" % (k % 2)
                                S.add("dve", lambda e, k=k, gi=gi, yt_=yt_: e.tensor_tensor(out=yt_.rearrange("p (h q) -> p h q", h=4), in0=ps[k][:, 256:512].rearrange("p (h q) -> p h q", h=4),
                                      in1=bc(dec_[:, 32 + gi * 4:32 + gi * 4 + 4].unsqueeze(2), [128, 4, 64]), op=ALU.mult), reads=[psk[k], ke], writes=[ky])
                                S.add("dve", lambda e, k=k, gi=gi, yt_=yt_: e.tensor_tensor(out=ych[:, gi * 256:(gi + 1) * 256], in0=ps[k][:, 0:256], in1=yt_, op=ALU.add),
                                      reads=[psk[k], ky], writes=["ych"])
                            for k, gi in enumerate(gs):
                                sb = k
                                mm(ps[sb][:, 0:256], Btok_[:, gi * 128:(gi + 1) * 128], xdw_[:, gi * 256:(gi + 1) * 256], True, True, [kb, kw], [psk[sb]])
                            for k, gi in enumerate(gs):
                                sb = k
                                sg_ = Sst[:, gi * 256:(gi + 1) * 256]
                                S.add("pool", lambda e, gi=gi, sg_=sg_: e.tensor_tensor(out=sg_.rearrange("p (h q) -> p h q", h=4), in0=sg_.rearrange("p (h q) -> p h q", h=4),
                                      in1=bc(dec_[:, 64 + gi * 4:64 + gi * 4 + 4].unsqueeze(2), [128, 4, 64]), op=ALU.mult), reads=[SK + str(gi), ke], writes=[SK + str(gi)])
                                S.add("dve", lambda e, sg_=sg_, sb=sb, k=k: e.tensor_tensor(out=sg_, in0=sg_, in1=ps[sb][:, 0:256], op=ALU.add),
                                      reads=[SK + str(gi), psk[sb]], writes=[SK + str(gi)])
                        if d == 0:
                            S.add("dve", lambda e: e.tensor_tensor(out=ych, in0=ych, in1=yft, op=ALU.add), reads=["ych", "yft"], writes=["ych"])
                            S.store("sp", lambda e, tc0=tc0: e.dma_start(out=YF[tc0:tc0 + 128, :], in_=ych), "ych", ["YF"])
                        else:
                            S.add("dve", lambda e: e.tensor_tensor(out=ych, in0=ych, in1=yft, op=ALU.add), reads=["ych", "yft"], writes=["ych"])
                            S.add("pool", lambda e: e.tensor_tensor(out=ych, in0=ych, in1=zt, op=ALU.mult), reads=["ych", "zt"], writes=["ych"])
                            S.add("act", lambda e: e.activation(out=yft, in_=ych, func=AF.Square, accum_out=ss1[:, 0:1]), reads=["ych"], writes=["yft", "ss1"])
                            S.add("act", lambda e: e.activation(out=ss1[:, 0:1], in_=ss1[:, 0:1], func=AF.Sqrt, scale=1.0 / 2048, bias=EPS), reads=["ss1"], writes=["ss1"])
                            S.add("dve", lambda e: e.reciprocal(out=ss1[:, 0:1], in_=ss1[:, 0:1]), reads=["ss1"], writes=["ss1"])
                            S.add("dve", lambda e: e.scalar_tensor_tensor(out=ych, in0=ych, scalar=ss1[:, 0:1], in1=ngrow, op0=ALU.mult, op1=ALU.mult),
                                  reads=["ych", "ss1", "ngrow"], writes=["ych"])
                            for q in range(4):
                                b = 5 + q % 2
                                for j in range(4):
                                    tr(ps[b][:, j * 128:(j + 1) * 128], ych[:, (q * 4 + j) * 128:(q * 4 + j + 1) * 128], ["ych"], [psk[b]])
                                evac("act" if q % 2 == 0 else "dve", zt[:, q * 512:(q + 1) * 512], ps[b][:, :], [psk[b]], ["zt"])
                            S.store("sp", lambda e, tc0=tc0: e.dma_start(out=YST[:, tc0:tc0 + 128].rearrange("(fc p) n -> p fc n", p=128),
                                    in_=zt.rearrange("p (fc n) -> p fc n", n=128)), "zt", ["YST"])
                    for cc in range(nch):
                        chunk_body(cc)
                    if not lat:
                        for q in range(4):
                            b = 5 + q % 2
                            for j in range(4):
                                tr(ps[b][:, j * 128:(j + 1) * 128], Sst[:, (q * 4 + j) * 128:(q * 4 + j + 1) * 128].bitcast(F32), [SK + str(k) for k in range(8)], [psk[b]])
                            evac("act", xs_tok[:, q * 512:(q + 1) * 512], ps[b][:, :], [psk[b]], ["xs_tok"])
                        S.store("sp", lambda e, si=si, d=d: e.dma_start(out=new_ssd[si, l, d].rearrange("(j p) n -> p j n", p=128),
                                in_=xs_tok.rearrange("p (j n) -> p j n", n=128)), "xs_tok")

            for si in GROUPS[g][2]:
                for d in range(2):
                    dir_body(si, *SEQS[si], d)

        def phaseFG(l, g):
            S.barrier()
            AR.reset(); ARr.reset()
            g0, gn, _ = GROUPS[g]
            kind = g
            NT = 1024
            R1 = ARr.alloc([11, NT]); R2 = ARr.alloc([8, NT])
            R1f = R1.rearrange("p a b -> p (a b)")
            BIN = R1f[:, 0:8 * NT].rearrange("p (a b) -> p a b", b=NT)
            sq = R1f[:, 0:4096].rearrange("p (a b) -> p a b", b=512)
            wb = [ARr.alloc([2048]) for _ in range(4)]
            xT = AR.alloc([8, NT]); Gt = [AR.alloc([NT]) for _ in range(2)]; rstd = AR.alloc([512])
            xn = [AR.alloc([512]) for _ in range(2)]; tmp = [AR.alloc([512]) for _ in range(2)]; xtok = [AR.alloc([1024]) for _ in range(2)]
            last = (l == n_layers - 1)
            wcnt = [0]; tcnt = [0]

            WL = []
            wstate = {"issued": 0, "next": 0, "base": 0}
            LA = 2

            def wnext():
                j = wstate["next"]; wstate["next"] += 1
                while wstate["issued"] < min(len(WL), j + LA + 1):
                    k = wstate["issued"]; wstate["issued"] += 1
                    sk = (wstate["base"] + k) % 4
                    for (vf, src) in WL[k]:
                        wload(vf(wb[sk]), src, "wb%d" % sk)
                return (wstate["base"] + j) % 4

            def build_wl():
                WL.clear()
                v8 = lambda n0, n1: (lambda t: t.rearrange("p (kc n) -> p kc n", kc=8)[:, :, n0:n1])
                kcp = lambda a: a.rearrange("(kc p) n -> p kc n", p=128)
                for (src, r0, wsrc, k0, grow0) in units:
                    for blk in range(4):
                        WL.append([(v8(0, 256), kcp(wsrc[l][k0:k0 + 1024, blk * 256:(blk + 1) * 256]))])
                for blk in range(4):
                    WL.append([(v8(0, 256), kcp(w_merge[l][:, blk * 256:(blk + 1) * 256]))])
                for half in range(2):
                    for p0 in range(0, 11, 2):
                        c0 = half * 11 + p0
                        npc = min(2, 11 - p0)
                        WL.append([(v8(0, npc * 128), kcp(ffn_w1[l][:, c0 * 128:(c0 + npc) * 128]))])
                        WL.append([(v8(0, npc * 128), kcp(ffn_w1[l][:, DFF + c0 * 128:DFF + (c0 + npc) * 128]))])
                    for fo in range(8):
                        WL.append([((lambda t: t[:, 0:1408].rearrange("p (kc n) -> p kc n", kc=11)), kcp(ffn_w2[l][half * 1408:(half + 1) * 1408, fo * 128:(fo + 1) * 128]))])
                wstate["base"] = (wstate["base"] + wstate["next"]) % 4
                wstate["issued"] = 0; wstate["next"] = 0

            def nps():
                b = rr["ps"] % 4; rr["ps"] += 1
                return b

            units = ((AOS, 0, w_attn_o, 0, FM_GT), (CVS, 0, w_conv_o, 0, FM_GT + 1024), (YST, 0, w_ssd_o, 0, FM_GT + 2048), (YST, 1024, w_ssd_o, 1024, FM_GT + 2048))
            for t in range(gn // NT):
                t0 = g0 + t * NT
                build_wl()
                load_xT(l, t0, NT, xT, xtok, "xT")
                gcnt = 0
                for ui, (src, r0, wsrc, k0, grow0) in enumerate(units):
                    S.load("pool", lambda e, src=src, r0=r0, t0=t0: e.dma_start(out=BIN, in_=src[r0:r0 + 1024, t0:t0 + NT].rearrange("(kc p) n -> p kc n", p=128)), "R1", ["AOS", "CVS", "YST"])
                    for blk in range(4):
                        s = wnext()
                        wv = wb[s].rearrange("p (kc n) -> p kc n", kc=8)
                        for fo2 in range(2):
                            fa = blk * 2 + fo2
                            gq = gcnt % 2; gcnt += 1
                            S.load("sp", lambda e, gq=gq, fa=fa, grow0=grow0, t0=t0: e.dma_start(out=Gt[gq], in_=PFM[grow0 + fa * 128:grow0 + (fa + 1) * 128, t0:t0 + NT]), "Gt%d" % gq, ["PFM"])
                            for sub in range(2):
                                b = nps()
                                sl = slice(sub * 512, (sub + 1) * 512)
                                for kc in range(8):
                                    mm(ps[b][:, :], wv[:, kc, fo2 * 128:(fo2 + 1) * 128], BIN[:, kc, sl], kc == 0, kc == 7, ["wb%d" % s, "R1"], [psk[b]])
                                if ui == 0:
                                    S.add("dve", lambda e, b=b, fa=fa, sl=sl, gq=gq: e.tensor_tensor(out=R2[:, fa, sl], in0=ps[b][:, :], in1=Gt[gq][:, sl], op=ALU.mult), reads=[psk[b], "Gt%d" % gq], writes=["R2"])
                                else:
                                    tq = tcnt[0] % 2; tcnt[0] += 1
                                    S.add("dve", lambda e, b=b, sl=sl, gq=gq, tq=tq: e.tensor_tensor(out=tmp[tq], in0=ps[b][:, :], in1=Gt[gq][:, sl], op=ALU.mult), reads=[psk[b], "Gt%d" % gq], writes=["tmp%d" % tq])
                                    S.add("pool", lambda e, fa=fa, sl=sl, tq=tq: e.tensor_tensor(out=R2[:, fa, sl], in0=R2[:, fa, sl], in1=tmp[tq], op=ALU.add), reads=["R2", "tmp%d" % tq], writes=["R2"])
                for blk in range(4):
                    s = wnext()
                    wv = wb[s].rearrange("p (kc n) -> p kc n", kc=8)
                    for fo2 in range(2):
                        fa = blk * 2 + fo2
                        for sub in range(2):
                            b = nps()
                            sl = slice(sub * 512, (sub + 1) * 512)
                            for kc in range(8):
                                mm(ps[b][:, :], wv[:, kc, fo2 * 128:(fo2 + 1) * 128], R2[:, kc, sl], kc == 0, kc == 7, ["wb%d" % s, "R2"], [psk[b]])
                            S.add("dve", lambda e, b=b, fa=fa, sl=sl: e.scalar_tensor_tensor(out=xT[:, fa, sl], in0=ps[b][:, :], scalar=G1[:, fa, kind:kind + 1], in1=xT[:, fa, sl], op0=ALU.mult, op1=ALU.add),
                                  reads=[psk[b], "modv", "xT"], writes=["xT"])
                for sub in range(2):
                    sl = slice(sub * 512, (sub + 1) * 512)
                    ln_fm(xT[:, :, sl], R2[:, :, sl], sq, 512, A2, B2, kind, 5, ["xT"], "R2", rstd, xn, sqkey="R1")
                for half in range(2):
                    fcs = list(range(half * 11, half * 11 + 11))
                    for p0 in range(0, 11, 2):
                        pc = fcs[p0:p0 + 2]
                        npc = len(pc)
                        s1 = wnext(); s2 = wnext()
                        wg = wb[s1].rearrange("p (kc n) -> p kc n", kc=8); wu = wb[s2].rearrange("p (kc n) -> p kc n", kc=8)
                        for j, fc in enumerate(pc):
                            fl = fc - half * 11
                            for sub in range(2):
                                sl = slice(sub * 512, (sub + 1) * 512)
                                ba = nps(); bb = nps()
                                for kc in range(8):
                                    mm(ps[ba][:, :], wg[:, kc, j * 128:(j + 1) * 128], R2[:, kc, sl], kc == 0, kc == 7, ["wb%d" % s1, "R2"], [psk[ba]])
                                for kc in range(8):
                                    mm(ps[bb][:, :], wu[:, kc, j * 128:(j + 1) * 128], R2[:, kc, sl], kc == 0, kc == 7, ["wb%d" % s2, "R2"], [psk[bb]])
                                tq = tcnt[0] % 2; tcnt[0] += 1
                                S.add("act", lambda e, ba=ba, tq=tq: e.activation(out=tmp[tq], in_=ps[ba][:, :], func=AF.Silu), reads=[psk[ba]], writes=["tmp%d" % tq])
                                S.add("dve", lambda e, bb=bb, fl=fl, sl=sl, tq=tq: e.tensor_tensor(out=R1[:, fl, sl], in0=tmp[tq], in1=ps[bb][:, :], op=ALU.mult), reads=["tmp%d" % tq, psk[bb]], writes=["R1"])
                    for fo in range(8):
                        s = wnext()
                        wv = wb[s][:, 0:1408].rearrange("p (kc n) -> p kc n", kc=11)
                        for sub in range(2):
                            sl = slice(sub * 512, (sub + 1) * 512)
                            b = nps()
                            for kc in range(11):
                                mm(ps[b][:, :], wv[:, kc, :], R1[:, kc, sl], kc == 0, kc == 10, ["wb%d" % s, "R1"], [psk[b]])
                            S.add("dve", lambda e, b=b, fo=fo, sl=sl: e.scalar_tensor_tensor(out=xT[:, fo, sl], in0=ps[b][:, :], scalar=G2[:, fo, kind:kind + 1], in1=xT[:, fo, sl], op0=ALU.mult, op1=ALU.add),
                                  reads=[psk[b], "modv", "xT"], writes=["xT"])
                if not last:
                    S.store("sp", lambda e, t0=t0: e.dma_start(out=XST[:, t0:t0 + NT].rearrange("(kc p) n -> p kc n", p=128), in_=xT), "xT", ["XST"])
                else:
                    for i in range(NT // 128):
                        for h in range(2):
                            b = 6 + h
                            for j in range(4):
                                tr(ps[b][:, j * 128:(j + 1) * 128], xT[:, h * 4 + j, i * 128:(i + 1) * 128], ["xT"], [psk[b]])
                            evac("act" if h == 0 else "dve", xtok[i % 2][:, h * 512:(h + 1) * 512], ps[b][:, :], [psk[b]], ["xtok%d" % (i % 2)])
                        S.store("sp", lambda e, i=i, t0=t0: e.dma_start(out=y_all[t0 + i * 128:t0 + (i + 1) * 128, :], in_=xtok[i % 2]), "xtok%d" % (i % 2))

        PH = {"A": phaseAB, "C": phaseC, "E": phaseE, "F": phaseFG}
        for l in range(n_layers):
            layer_setup(l)
            for g in range(2):
                for p in phases:
                    if p in PH:
                        PH[p](l, g)
        S.emit(st)
    return nc


def _host_consts():
    k = np.arange(128)[:, None]
    i = np.arange(128)[None, :]
    mats = [np.eye(128), (k <= i), (k > i), (k < i), (k >= i), np.ones((128, 128))]
    consts = np.concatenate([m.astype(np.float32) for m in mats], axis=1)
    rows = TS // 64
    row = np.repeat(np.arange(rows, dtype=np.float32), 64)
    col = np.tile(np.arange(64, dtype=np.float32), rows)
    inv = (1.0 / (np.float32(10000.0) ** (np.arange(0, 64, 2, dtype=np.float32) / np.float32(64)))).astype(np.float32)
    ang = np.concatenate([row[:, None] * inv, col[:, None] * inv], axis=-1).astype(np.float32)
    cs = np.cos(ang).astype(np.float32).reshape(16, 128, 64).transpose(1, 0, 2)
    sn = np.sin(ang).astype(np.float32).reshape(16, 128, 64).transpose(1, 0, 2)
    rope = np.ascontiguousarray(np.stack([cs, sn], axis=1)).astype(np.float32)
    return np.ascontiguousarray(consts), rope


def _prep(inp):
    f = lambda a: np.ascontiguousarray(np.asarray(a, dtype=np.float32))
    consts, rope = _host_consts()
    bro = lambda v: np.broadcast_to(np.asarray(v, np.float32)[None, :], (128, v.shape[-1]))
    small = []
    ngrow = []
    for l in range(NL):
        parts = [
            inp["ada_b"][l].reshape(48, 128).T,
            inp["norm1_g"][l].reshape(8, 128).T,
            inp["norm2_g"][l].reshape(8, 128).T,
            inp["conv_w"][l].reshape(3, 8, 128).transpose(2, 1, 0).reshape(128, 24),
            inp["ssd_conv_w"][l].reshape(3, 32, 128).transpose(2, 1, 0).reshape(128, 96),
            inp["ssd_conv_b"][l].reshape(32, 128).T,
            bro(inp["q_norm_g"][l]),
            bro(inp["k_norm_g"][l]),
            bro(inp["ssd_dt_bias"][l].reshape(64)),
            bro(inp["ssd_a_log"][l].reshape(64)),
            bro(inp["ssd_d"][l]),
        ]
        small.append(np.concatenate([np.asarray(p, np.float32) for p in parts], axis=1))
        ngrow.append(bro(inp["ssd_norm_g"][l]))
    small = f(np.stack(small))
    ngrow = f(np.stack(ngrow))
    shared = dict(ada_w=f(inp["ada_w"]), w_in=f(inp["w_in"]), w_attn_o=f(inp["w_attn_o"]), w_conv_o=f(inp["w_conv_o"]),
                  w_ssd_o=f(inp["w_ssd_o"]), w_merge=f(inp["w_merge"]), ffn_w1=f(inp["ffn_w1"]), ffn_w2=f(inp["ffn_w2"]),
                  small=small, ngrow=ngrow, consts=consts, rope=rope)
    maps = []
    for c in range(8):
        m = dict(shared)
        m["x_all"] = f(np.concatenate([inp["x_prompt"][4 * c:4 * c + 4].reshape(4 * TP, D), inp["x_sample"][c]], axis=0))
        m["c_fm"] = f(np.stack([inp["c_ctx"].reshape(8, 128).T, inp["c"][c].reshape(8, 128).T], axis=-1))
        m["ck"] = f(inp["cache_k"][c].reshape(NL, PAST, 256))
        m["cv"] = f(inp["cache_v"][c].reshape(NL, PAST, 256))
        m["s0"] = f(inp["state_ssd"][c].reshape(NL, 2, 2048, 128))
        maps.append(m)
    return maps


_NC_CACHE = {}


def kernel(**inputs):
    inp = {k: np.asarray(v) for k, v in inputs.items()}
    maps = _prep(inp)
    if "nc" not in _NC_CACHE:
        _NC_CACHE["nc"] = build()
    res = run_bass_kernel_spmd(_NC_CACHE["nc"], maps, core_ids=list(range(8)))
    r = res.results
    y_prompt = np.concatenate([r[c]["y_all"][:4 * TP].reshape(4, TP, D) for c in range(8)], axis=0)
    y_sample = np.stack([r[c]["y_all"][4 * TP:] for c in range(8)], axis=0)
    new_k = np.concatenate([r[c]["new_k"].reshape(4, NL, TP, 2, 128) for c in range(8)], axis=0)
    new_v = np.concatenate([r[c]["new_v"].reshape(4, NL, TP, 2, 128) for c in range(8)], axis=0)
    new_ssd = np.concatenate([r[c]["new_ssd"].reshape(4, NL, 2, 32, 64, 128) for c in range(8)], axis=0)
    return (y_prompt.astype(np.float32), y_sample.astype(np.float32), new_k.astype(np.float32),
            new_v.astype(np.float32), new_ssd.astype(np.float32))
```

```python
from contextlib import ExitStack
import numpy as np
import concourse.bass as bass
import concourse.mybir as mybir
from concourse.bass_utils import run_bass_kernel_spmd

F32 = mybir.dt.float32
F32R = mybir.dt.float32r
AF = mybir.ActivationFunctionType
ALU = mybir.AluOpType
AX = mybir.AxisListType

D = 1024
NL = 2
TP = 256
TS = 2048
PAST = 512
NTOK = 4 * TP + TS
IN_DIM = 13888
DFF = 2816
EPS = 1e-6
C_Q, C_K, C_V, C_BG, C_CG, C_CX, C_Z, C_XBC, C_DT, C_GT = 0, 1024, 1280, 1536, 2560, 3584, 4608, 6656, 10752, 10816
PT_W = 3648
FM_CONV, FM_XBC, FM_GT = 0, 3072, 7168
FM_ROWS = 10240
SEQS = [(0, 256, False), (256, 256, False), (512, 256, False), (768, 256, False), (1024, 2048, True)]
GROUPS = [(0, 1024, [0, 1, 2, 3]), (1024, 2048, [4])]


class Op:
    __slots__ = ("eng", "fn", "reads", "writes", "dsem", "deps", "signals", "sig_val", "idx", "is_dma", "hard")

    def __init__(self, eng, fn, reads, writes, dsem):
        self.eng = eng
        self.fn = fn
        self.reads = reads
        self.writes = writes
        self.dsem = dsem
        self.deps = []
        self.hard = set()
        self.signals = False
        self.sig_val = None
        self.is_dma = dsem is not None


class Sched:
    ENGS = ("pe", "act", "dve", "pool", "sp")

    def __init__(self, nc):
        self.nc = nc
        self.ops = []
        self.last_w = {}
        self.readers = {}
        self.last_eng = {}
        self.last_dma = {}
        self.pending = {}

    def add(self, eng, fn, reads=(), writes=(), dsem=None):
        op = Op(eng, fn, tuple(reads), tuple(writes), dsem)
        op.idx = len(self.ops)
        deps = set()
        for r in op.reads:
            w = self.last_w.get(r)
            if w is not None:
                deps.add(w)
        for r in op.writes:
            w = self.last_w.get(r)
            if w is not None:
                deps.add(w)
            for rd in self.readers.get(r, {}).values():
                deps.update(rd)
        if eng in self.pending:
            op.hard = self.pending.pop(eng)
            deps |= op.hard
        deps.discard(op.idx)
        op.deps = sorted(deps)
        for r in op.reads:
            rd = self.readers.setdefault(r, {})
            if op.is_dma:
                rd.setdefault("dma", []).append(op.idx)
            else:
                rd[eng] = [op.idx]
        for r in op.writes:
            self.last_w[r] = op.idx
            self.readers[r] = {}
        if op.is_dma:
            self.last_dma[dsem] = op.idx
        else:
            self.last_eng[eng] = op.idx
        self.ops.append(op)
        return op

    def load(self, eng, fn, dst, srcs=()):
        return self.add(eng, fn, reads=srcs, writes=[dst], dsem="L:" + str(dst))

    def store(self, eng, fn, src, dsts=()):
        return self.add(eng, fn, reads=[src], writes=dsts, dsem="S:" + str(src))

    def barrier(self):
        deps = set(self.last_eng.values()) | set(self.last_dma.values())
        for e in self.ENGS:
            self.pending[e] = set(deps) | self.pending.get(e, set())

    def emit(self, stack):
        nc = self.nc
        ops = self.ops
        need = []
        for op in ops:
            nd = []
            for d in op.deps:
                dop = ops[d]
                if dop.eng == op.eng and not dop.is_dma and not op.is_dma:
                    if op.eng == "pe":
                        continue
                    if d not in op.hard and not (set(dop.writes) & set(op.reads)):
                        continue
                nd.append(d)
                dop.signals = True
            need.append(nd)
        esem = {e: stack.enter_context(nc.semaphore("s_" + e)) for e in self.ENGS}
        dsems = {}
        cnt = {e: 0 for e in self.ENGS}
        dcnt = {}
        for op in ops:
            if op.is_dma:
                if op.dsem not in dsems:
                    dsems[op.dsem] = stack.enter_context(nc.semaphore("d%d" % len(dsems)))
                    dcnt[op.dsem] = 0
                dcnt[op.dsem] += 16
                op.sig_val = (dsems[op.dsem], dcnt[op.dsem])
                op.signals = True
            elif op.signals:
                cnt[op.eng] += 1
                op.sig_val = (esem[op.eng], cnt[op.eng])
        per_eng = {e: [] for e in self.ENGS}
        for op in ops:
            per_eng[op.eng].append(op)
        block = stack.enter_context(nc.Block())
        final = {k: (dsems[k], dcnt[k]) for k in dsems}
        self.stats = dict(n_ops=len(ops), sig=dict(cnt), n_dsem=len(dsems), dmax=max(dcnt.values()) if dcnt else 0,
                          per_eng={e: len(per_eng[e]) for e in per_eng})

        def body(ename):
            def run(eng):
                waited = {}
                for op in per_eng[ename]:
                    for d in need[op.idx]:
                        sem, val = ops[d].sig_val
                        key = id(sem)
                        if waited.get(key, 0) >= val:
                            continue
                        waited[key] = val
                        eng.wait_ge(sem, val)
                    ins = op.fn(eng)
                    if op.signals:
                        ins.then_inc(op.sig_val[0], 16 if op.is_dma else 1)
                if ename == "sp":
                    for k, (sem, val) in final.items():
                        if waited.get(id(sem), 0) < val:
                            eng.wait_ge(sem, val)
            return run

        block.tensor(body("pe"))
        block.scalar(body("act"))
        block.vector(body("dve"))
        block.gpsimd(body("pool"))
        block.sync(body("sp"))


class Arena:
    def __init__(self, tile, size):
        self.t = tile
        self.size = size
        self.off = 0

    def reset(self):
        self.off = 0

    def alloc(self, shape, dt=None):
        n = int(np.prod(shape))
        assert self.off + n <= self.size, ("arena overflow", self.off, n, self.size)
        ap = self.t[:, self.off:self.off + n]
        self.off += n
        if len(shape) == 2:
            return ap.rearrange("p (a b) -> p a b", b=shape[1])
        if len(shape) == 3:
            return ap.rearrange("p (a b c) -> p a b c", b=shape[1], c=shape[2])
        return ap


def bc(ap, shape):
    return ap.broadcast_to(list(shape))


def build(n_layers=NL, phases="ACEF", debug=False):
    nc = bass.Bass("TRN2", target_bir_lowering=False)

    def din(name, shape):
        return nc.dram_tensor(name, list(shape), F32, kind="ExternalInput").ap()

    def dout(name, shape):
        return nc.dram_tensor(name, list(shape), F32, kind="ExternalOutput").ap()

    def dscr(name, shape):
        return nc.dram_tensor(name, list(shape), F32, kind=("ExternalOutput" if debug else "Internal")).ap()

    x_allT = din("x_allT", [D, NTOK])
    c_fm = din("c_fm", [128, 8, 2])
    ck = din("ck", [NL, PAST, 256])
    cv = din("cv", [NL, PAST, 256])
    s0 = din("s0", [NL, 2, 2048, 128])
    ada_w = din("ada_w", [NL, D, 6 * D])
    w_in = din("w_in", [NL, D, IN_DIM])
    w_attn_o = din("w_attn_o", [NL, D, D])
    w_conv_o = din("w_conv_o", [NL, D, D])
    w_ssd_o = din("w_ssd_o", [NL, 2 * D, D])
    w_merge = din("w_merge", [NL, D, D])
    ffn_w1 = din("ffn_w1", [NL, D, 2 * DFF])
    ffn_w2 = din("ffn_w2", [NL, DFF, D])
    SM_W = 48 + 8 + 8 + 24 + 96 + 32 + 128 + 128 + 64 + 64 + 32
    small = din("small", [NL, 128, SM_W])
    ngrow_d = din("ngrow", [NL, 128, 2048])
    consts = din("consts", [128, 6 * 128])
    rope = din("rope", [128, 2, 16, 64])

    y_allT = dout("y_allT", [D, NTOK])
    new_k = dout("new_k", [4, NL, TP, 256])
    new_v = dout("new_v", [4, NL, TP, 256])
    new_ssd = dout("new_ssd", [4, NL, 2, 2048, 128])

    XST = dscr("XST", [D, NTOK])
    PTOK = dscr("PTOK", [NTOK, PT_W])
    PFM = dscr("PFM", [FM_ROWS, NTOK])
    XBCS = dscr("XBCS", [4096, NTOK])
    CVS = dscr("CVS", [D, NTOK])
    AOS = dscr("AOS", [D, NTOK])
    YF = dscr("YF", [NTOK, 2048])
    YST = dscr("YST", [2048, NTOK])

    with ExitStack() as st:
        ARENA_N = 15 * 1024
        arena_t = st.enter_context(nc.sbuf_tensor("arena", [128, ARENA_N], F32))
        AR = Arena(arena_t, ARENA_N)
        ARENAR_N = 28 * 1024
        arenar_t = st.enter_context(nc.sbuf_tensor("arenar", [128, ARENAR_N], F32R))
        ARr = Arena(arenar_t, ARENAR_N)
        PERS_N = 6200
        pers_t = st.enter_context(nc.sbuf_tensor("pers", [128, PERS_N], F32))
        PR = Arena(pers_t, PERS_N)
        persr_t = st.enter_context(nc.sbuf_tensor("persr", [128, 400], F32R))
        PRr = Arena(persr_t, 400)
        ps = [st.enter_context(nc.psum_tensor("ps%d" % i, [128, 512], F32)) for i in range(8)]
        S = Sched(nc)

        cst = PR.alloc([6, 128])
        ident, Mle, Mgt, Mlt, Mge, ones = [cst[:, i, :] for i in range(6)]
        cstr = PRr.alloc([3, 128])
        Mgt_r, Mlt_r, ones_r = [cstr[:, i, :] for i in range(3)]
        ropet = PR.alloc([2, 16, 64])
        csil = PRr.alloc([8, 2])
        craw = PR.alloc([8, 2])
        sm = PR.alloc([SM_W])
        o = 0
        adab = sm[:, o:o + 48]; o += 48
        n1g = sm[:, o:o + 8]; o += 8
        n2g = sm[:, o:o + 8]; o += 8
        convw = sm[:, o:o + 24].rearrange("p (a b) -> p a b", b=3); o += 24
        sconvw = sm[:, o:o + 96].rearrange("p (a b) -> p a b", b=3); o += 96
        sconvb = sm[:, o:o + 32]; o += 32
        qgrow = sm[:, o:o + 128]; o += 128
        kgrow = sm[:, o:o + 128]; o += 128
        dtbrow = sm[:, o:o + 64]; o += 64
        alogrow = sm[:, o:o + 64]; o += 64
        drow = sm[:, o:o + 32]; o += 32
        arow = PR.alloc([64])
        nconvw = PR.alloc([8, 3]); nsconvw = PR.alloc([32, 3])
        modv = PR.alloc([48, 2])
        A1 = PR.alloc([8, 2]); A2 = PR.alloc([8, 2])
        ngrow = PR.alloc([2048])
        B1 = modv[:, 0:8, :]; G1 = modv[:, 16:24, :]; B2 = modv[:, 24:32, :]; G2 = modv[:, 40:48, :]

        psk = ["ps%d" % i for i in range(8)]

        S.load("sp", lambda e: e.dma_start(out=cst, in_=consts.rearrange("p (a b) -> p a b", b=128)), "cst")
        S.load("sp", lambda e: e.dma_start(out=ropet, in_=rope), "rope")
        S.load("sp", lambda e: e.dma_start(out=craw, in_=c_fm), "craw")
        S.add("act", lambda e: e.activation(out=csil, in_=craw, func=AF.Silu), reads=["craw"], writes=["csil"])
        S.add("act", lambda e: e.activation(out=cstr[:, 0:2, :], in_=cst[:, 2:4, :], func=AF.Copy), reads=["cst"], writes=["cstr"])
        S.add("act", lambda e: e.activation(out=cstr[:, 2, :], in_=cst[:, 5, :], func=AF.Copy), reads=["cst"], writes=["cstr"])

        rr = {"ps": 0, "sg": 0, "wb": 0}

        def evac(eng, out, in_, reads, writes, func=None, scale=None, bias=None):
            if eng == "act":
                kw = {}
                if scale is not None:
                    kw["scale"] = scale
                if bias is not None:
                    kw["bias"] = bias
                S.add("act", lambda e: e.activation(out=out, in_=in_, func=(func or AF.Copy), **kw), reads=reads, writes=writes)
            else:
                assert func is None
                if scale is not None:
                    S.add("dve", lambda e: e.tensor_scalar(out=out, in0=in_, scalar1=scale, scalar2=bias, op0=ALU.mult, op1=ALU.add), reads=reads, writes=writes)
                else:
                    S.add("dve", lambda e: e.tensor_copy(out=out, in_=in_), reads=reads, writes=writes)

        def mm(out, lhsT, rhs, start, stop, reads, writes):
            S.add("pe", lambda e: e.matmul(out, lhsT=lhsT, rhs=rhs, start=start, stop=stop), reads=reads, writes=writes)

        def tr(out, in_, reads, writes):
            S.add("pe", lambda e: e.transpose(out=out, in_=in_, identity=ident), reads=list(reads) + ["cst"], writes=writes)

        def wload(dst, src, key):
            S.load("pool", lambda e: e.dma_start(out=dst, in_=src), key)

        def layer_setup(l):
            S.barrier()
            AR.reset(); ARr.reset()
            S.load("sp", lambda e: e.dma_start(out=sm, in_=small[l]), "sm")
            S.load("sp", lambda e: e.dma_start(out=ngrow, in_=ngrow_d[l]), "ngrow")
            S.add("dve", lambda e: e.tensor_scalar(out=nconvw, in0=convw, scalar1=-1.0, scalar2=None, op0=ALU.mult), reads=["sm"], writes=["nconvw"])
            S.add("dve", lambda e: e.tensor_scalar(out=nsconvw, in0=sconvw, scalar1=-1.0, scalar2=None, op0=ALU.mult), reads=["sm"], writes=["nconvw"])
            S.add("act", lambda e: e.activation(out=arow, in_=alogrow, func=AF.Exp), reads=["sm"], writes=["arow"])
            S.add("dve", lambda e: e.tensor_scalar(out=arow, in0=arow, scalar1=-1.0, scalar2=None, op0=ALU.mult), reads=["arow"], writes=["arow"])
            wb = [ARr.alloc([8, 512]) for _ in range(2)]
            pm = ps[0][:, 0:96].rearrange("p (a b) -> p a b", b=2)
            for blk in range(12):
                s = blk % 2
                wload(wb[s], ada_w[l][:, blk * 512:(blk + 1) * 512].rearrange("(kc p) n -> p kc n", p=128), "wb%d" % s)
                for fo in range(4):
                    fc = blk * 4 + fo
                    for kc in range(8):
                        mm(pm[:, fc, :], wb[s][:, kc, fo * 128:(fo + 1) * 128], csil[:, kc, :], (blk == 0 and fo == 0 and kc == 0), kc == 7,
                           ["wb%d" % s, "csil"], ["ps0"])
            S.add("dve", lambda e: e.tensor_tensor(out=modv, in0=pm, in1=bc(adab.unsqueeze(2), [128, 48, 2]), op=ALU.add), reads=["ps0", "sm"], writes=["modv"])
            for (Ax, ng, lo) in ((A1, n1g, 8), (A2, n2g, 32)):
                S.add("dve", lambda e, Ax=Ax, lo=lo: e.tensor_scalar(out=Ax, in0=modv[:, lo:lo + 8, :], scalar1=1.0, scalar2=None, op0=ALU.add), reads=["modv"], writes=["A12"])
                S.add("dve", lambda e, Ax=Ax, ng=ng: e.tensor_tensor(out=Ax, in0=Ax, in1=bc(ng.unsqueeze(2), [128, 8, 2]), op=ALU.mult), reads=["A12", "sm"], writes=["A12"])

        def ln_fm(xT, hT, sq, n, A, B, kind, pbank, keys_in, key_out, rstd, xn, sqkey="sq"):
            S.add("act", lambda e: e.activation(out=sq[:, :, 0:n], in_=xT[:, :, 0:n], func=AF.Square), reads=keys_in, writes=[sqkey])
            for kc in range(8):
                mm(ps[pbank][:, 0:n], ones_r, sq[:, kc, 0:n], kc == 0, kc == 7, [sqkey, "cstr"], [psk[pbank]])
            S.add("act", lambda e: e.activation(out=rstd[:, 0:n], in_=ps[pbank][:, 0:n], func=AF.Ln, scale=1.0 / D, bias=EPS), reads=[psk[pbank]], writes=["rstd"])
            S.add("act", lambda e: e.activation(out=rstd[:, 0:n], in_=rstd[:, 0:n], func=AF.Exp, scale=-0.5), reads=["rstd"], writes=["rstd"])
            for kc in range(8):
                eng = "dve" if kc % 2 == 0 else "pool"
                S.add(eng, lambda e, kc=kc: e.tensor_tensor(out=xn[kc % 2][:, 0:n], in0=xT[:, kc, 0:n], in1=rstd[:, 0:n], op=ALU.mult),
                      reads=list(keys_in) + ["rstd"], writes=["xn%d" % (kc % 2)])
                evac("act" if kc % 2 == 0 else "dve", hT[:, kc, 0:n], xn[kc % 2][:, 0:n], ["xn%d" % (kc % 2), "A12", "modv"], [key_out],
                     func=(AF.Identity if kc % 2 == 0 else None), scale=A[:, kc, kind:kind + 1], bias=B[:, kc, kind:kind + 1])

        def load_xT(l, t0, n, xT, xtok, key):
            src = x_allT if l == 0 else XST
            S.load("sp", lambda e: e.dma_start(out=xT[:, :, 0:n], in_=src[:, t0:t0 + n].rearrange("(kc p) n -> p kc n", p=128)), key, ["XST"])

        def conv3(x, acc, n, nseg, w, nw, rk, wk):
            w0, w1, w2 = w[:, 0:1], w[:, 1:2], w[:, 2:3]
            S.add("dve", lambda e: e.tensor_scalar(out=acc[:, 0:n], in0=x[:, 0:n], scalar1=w1, scalar2=None, op0=ALU.mult), reads=[rk, "sm"], writes=[wk])
            S.add("dve", lambda e: e.scalar_tensor_tensor(out=acc[:, 1:n], in0=x[:, 0:n - 1], scalar=w0, in1=acc[:, 1:n], op0=ALU.mult, op1=ALU.add), reads=[rk, "sm", wk], writes=[wk])
            S.add("dve", lambda e: e.scalar_tensor_tensor(out=acc[:, 0:n - 1], in0=x[:, 1:n], scalar=w2, in1=acc[:, 0:n - 1], op0=ALU.mult, op1=ALU.add), reads=[rk, "sm", wk], writes=[wk])
            if nseg > 1:
                T = n // nseg
                av = acc[:, 0:n].rearrange("p (s t) -> p s t", t=T)
                xv = x[:, 0:n].rearrange("p (s t) -> p s t", t=T)
                S.add("dve", lambda e: e.scalar_tensor_tensor(out=av[:, 1:nseg, 0], in0=xv[:, 0:nseg - 1, T - 1], scalar=nw[:, 0:1], in1=av[:, 1:nseg, 0], op0=ALU.mult, op1=ALU.add),
                      reads=[rk, "nconvw", wk], writes=[wk])
                S.add("dve", lambda e: e.scalar_tensor_tensor(out=av[:, 0:nseg - 1, T - 1], in0=xv[:, 1:nseg, 0], scalar=nw[:, 2:3], in1=av[:, 0:nseg - 1, T - 1], op0=ALU.mult, op1=ALU.add),
                      reads=[rk, "nconvw", wk], writes=[wk])

        def phaseAB(l, g):
            S.barrier()
            AR.reset(); ARr.reset()
            g0, gn, _ = GROUPS[g]
            kind = g
            HT = ARr.alloc([8, 2048])
            wb = [ARr.alloc([8, 512]) for _ in range(2)]
            xT = AR.alloc([8, 512])
            sq = ARr.alloc([8, 512])
            rstd = AR.alloc([512])
            xtok = [AR.alloc([1024]) for _ in range(2)]
            xn = [AR.alloc([512]) for _ in range(2)]
            nt = gn // 512

            def conv_wloads(kc, s):
                for j, c0 in enumerate((C_CX, C_CG, C_BG)):
                    wload(wb[s][:, :, j * 128:(j + 1) * 128], w_in[l][:, c0 + kc * 128:c0 + (kc + 1) * 128].rearrange("(kc p) n -> p kc n", p=128), "wb%d" % s)

            conv_wloads(0, 0); conv_wloads(1, 1)
            for t in range(nt):
                load_xT(l, g0 + t * 512, 512, xT, xtok, "xT")
                ln_fm(xT, HT[:, :, t * 512:(t + 1) * 512], sq, 512, A1, B1, kind, 5, ["xT"], "HT%d" % t, rstd, xn)
            S.barrier()
            AR.reset()
            sg = [AR.alloc([512]) for _ in range(4)]
            G = [AR.alloc([2048]) for _ in range(4)]
            acc = [AR.alloc([2048]) for _ in range(2)]
            segs = [(SEQS[si][0] - g0, SEQS[si][1]) for si in GROUPS[g][2]]
            wc = [0]

            def proj_chunk(wv, dst, dkey, wkey):
                for t in range(nt):
                    b = rr["ps"] % 4; rr["ps"] += 1
                    for kc in range(8):
                        mm(ps[b][:, :], wv[:, kc, :], HT[:, kc, t * 512:(t + 1) * 512], kc == 0, kc == 7, ["HT%d" % t, wkey], [psk[b]])
                    evac("act", dst[:, t * 512:(t + 1) * 512], ps[b][:, :], [psk[b]], [dkey])

            nseg = len(segs)

            def proj_mul(wv, dst, dkey, mul, mkey, wkey):
                for t in range(nt):
                    b = rr["ps"] % 4; rr["ps"] += 1
                    tsl = slice(t * 512, (t + 1) * 512)
                    for kc in range(8):
                        mm(ps[b][:, :], wv[:, kc, :], HT[:, kc, tsl], kc == 0, kc == 7, ["HT%d" % t, wkey], [psk[b]])
                    S.add("dve", lambda e, b=b, tsl=tsl: e.tensor_tensor(out=dst[:, tsl], in0=ps[b][:, :], in1=mul[:, tsl], op=ALU.mult), reads=[psk[b], mkey], writes=[dkey])

            for kc in range(8):
                s = wc[0] % 2; wc[0] += 1
                if kc >= 2:
                    conv_wloads(kc, s)
                Gcx, cxk = G[2 * (kc % 2)], "G%d" % (2 * (kc % 2))
                Gu, uk = G[2 * (kc % 2) + 1], "G%d" % (2 * (kc % 2) + 1)
                a_, ak = acc[kc % 2], "acc%d" % (kc % 2)
                proj_chunk(wb[s][:, :, 0:128], Gcx, cxk, "wb%d" % s)
                proj_mul(wb[s][:, :, 128:256], Gu, uk, Gcx, cxk, "wb%d" % s)
                conv3(Gu, a_, gn, nseg, convw[:, kc, :], nconvw[:, kc, :], uk, ak)
                proj_mul(wb[s][:, :, 256:384], a_, ak, a_, ak, "wb%d" % s)
                S.store("sp", lambda e, a_=a_, kc=kc: e.dma_start(out=CVS[kc * 128:(kc + 1) * 128, g0:g0 + gn], in_=a_[:, 0:gn]), ak, ["CVS"])
            for blk in range(8):
                s = wc[0] % 2; wc[0] += 1
                wload(wb[s], w_in[l][:, C_XBC + blk * 512:C_XBC + (blk + 1) * 512].rearrange("(kc p) n -> p kc n", p=128), "wb%d" % s)
                for fo in range(4):
                    fc = blk * 4 + fo
                    gi = fc % 4
                    a_, ak = acc[fc % 2], "acc%d" % (fc % 2)
                    proj_chunk(wb[s][:, :, fo * 128:(fo + 1) * 128], G[gi], "G%d" % gi, "wb%d" % s)
                    conv3(G[gi], a_, gn, nseg, sconvw[:, fc, :], nsconvw[:, fc, :], "G%d" % gi, ak)
                    S.add("act", lambda e, a_=a_, fc=fc: e.activation(out=a_[:, 0:gn], in_=a_[:, 0:gn], func=AF.Silu, bias=sconvb[:, fc:fc + 1]), reads=[ak, "sm"], writes=[ak])
                    S.store("sp", lambda e, a_=a_, fc=fc: e.dma_start(out=XBCS[fc * 128:(fc + 1) * 128, g0:g0 + gn], in_=a_[:, 0:gn]), ak, ["XBCS"])
            jobs = []
            for b in range(3):
                jobs.append((b * 512, 512, "tok", b * 512, None))
            for b in range(4):
                jobs.append((C_Z + b * 512, 512, "tok", 1536 + b * 512, AF.Silu))
            jobs.append((C_DT, 64, "tok", 3584, None))
            for b in range(6):
                jobs.append((C_GT + b * 512, 512, "fm", FM_GT + b * 512, AF.Sigmoid))
            for ji, (c0, ncol, mode, dest, func) in enumerate(jobs):
                s = wc[0] % 2; wc[0] += 1
                wload(wb[s][:, :, 0:ncol], w_in[l][:, c0:c0 + ncol].rearrange("(kc p) n -> p kc n", p=128), "wb%d" % s)
                if mode == "tok":
                    for tt in range(gn // 128):
                        b = rr["ps"] % 4; rr["ps"] += 1
                        q = rr["sg"] % 4; rr["sg"] += 1
                        for kc in range(8):
                            lh = HT[:, kc, tt * 128:(tt + 1) * 128]
                            rh = wb[s][:, kc, 0:ncol]
                            if ncol < 256:
                                lh = lh.bitcast(F32); rh = rh.bitcast(F32)
                            mm(ps[b][:, 0:ncol], lh, rh, kc == 0, kc == 7, ["HT%d" % (tt // 4), "wb%d" % s], [psk[b]])
                        use_act = (func is not None) or (tt % 2 == 0)
                        evac("act" if use_act else "dve", sg[q][:, 0:ncol], ps[b][:, 0:ncol], [psk[b]], ["sg%d" % q], func=func)
                        S.store("sp", lambda e, q=q, tt=tt, dest=dest, ncol=ncol: e.dma_start(
                            out=PTOK[g0 + tt * 128:g0 + (tt + 1) * 128, dest:dest + ncol], in_=sg[q][:, 0:ncol]), "sg%d" % q, ["PTOK"])
                else:
                    for fo in range(4):
                        for t in range(nt):
                            b = rr["ps"] % 4; rr["ps"] += 1
                            q = rr["sg"] % 4; rr["sg"] += 1
                            for kc in range(8):
                                mm(ps[b][:, :], wb[s][:, kc, fo * 128:(fo + 1) * 128], HT[:, kc, t * 512:(t + 1) * 512], kc == 0, kc == 7,
                                   ["HT%d" % t, "wb%d" % s], [psk[b]])
                            use_act = (func is not None) or ((fo + t) % 2 == 0)
                            evac("act" if use_act else "dve", sg[q], ps[b][:, :], [psk[b]], ["sg%d" % q], func=func)
                            r0 = dest + fo * 128
                            S.store("sp", lambda e, q=q, r0=r0, t=t: e.dma_start(
                                out=PFM[r0:r0 + 128, g0 + t * 512:g0 + (t + 1) * 512], in_=sg[q]), "sg%d" % q, ["PFM"])


        def rope_ops(src, dst, tmp, H, i, rk, wk):
            cs = bc(ropet[:, 0, i, :].unsqueeze(1), [128, H, 64])
            sn = bc(ropet[:, 1, i, :].unsqueeze(1), [128, H, 64])
            sv = src.rearrange("p (h d two) -> p h d two", h=H, two=2)
            dv = dst.rearrange("p (h d two) -> p h d two", h=H, two=2)
            tv = tmp[:, 0:H * 64].rearrange("p (h d) -> p h d", h=H)
            xe, xo, de, do = sv[:, :, :, 0], sv[:, :, :, 1], dv[:, :, :, 0], dv[:, :, :, 1]
            S.add("dve", lambda e: e.tensor_tensor(out=de, in0=xe, in1=cs, op=ALU.mult), reads=[rk, "rope"], writes=[wk])
            S.add("pool", lambda e: e.tensor_tensor(out=tv, in0=xo, in1=sn, op=ALU.mult), reads=[rk, "rope"], writes=["ropetmp"])
            S.add("dve", lambda e: e.tensor_tensor(out=de, in0=de, in1=tv, op=ALU.subtract), reads=[wk, "ropetmp"], writes=[wk])
            S.add("dve", lambda e: e.tensor_tensor(out=do, in0=xe, in1=sn, op=ALU.mult), reads=[rk, "rope"], writes=[wk])
            S.add("pool", lambda e: e.tensor_tensor(out=tv, in0=xo, in1=cs, op=ALU.mult), reads=[rk, "rope", wk], writes=["ropetmp"])
            S.add("dve", lambda e: e.tensor_tensor(out=do, in0=do, in1=tv, op=ALU.add), reads=[wk, "ropetmp"], writes=[wk])

        def qk_norm(src, dst, H, grow, sqk, ssq, rk, wk, part="ab", sk="ssq"):
            n = H * 128
            if "a" in part:
                S.add("pool", lambda e: e.tensor_tensor(out=sqk[:, 0:n], in0=src, in1=src, op=ALU.mult), reads=[rk], writes=["sqk"])
                S.add("dve", lambda e: e.tensor_reduce(out=ssq[:, 0:H], in_=sqk[:, 0:n].rearrange("p (h d) -> p h d", d=128), op=ALU.add, axis=AX.X), reads=["sqk"], writes=[sk])
            if "b" in part:
                S.add("act", lambda e: e.activation(out=ssq[:, 0:H], in_=ssq[:, 0:H], func=AF.Ln, scale=1.0 / 128, bias=EPS), reads=[sk], writes=[sk])
                S.add("act", lambda e: e.activation(out=ssq[:, 0:H], in_=ssq[:, 0:H], func=AF.Exp, scale=-0.5), reads=[sk], writes=[sk])
                d3 = dst.rearrange("p (h d) -> p h d", d=128)
                S.add("dve", lambda e: e.tensor_tensor(out=d3, in0=src.rearrange("p (h d) -> p h d", d=128), in1=bc(ssq[:, 0:H].unsqueeze(2), [128, H, 128]), op=ALU.mult), reads=[rk, sk], writes=[wk])
                S.add("pool", lambda e: e.tensor_tensor(out=d3, in0=d3, in1=bc(grow.unsqueeze(1), [128, H, 128]), op=ALU.mult), reads=[wk, "sm"], writes=[wk])

        def phaseC(l, g):
            S.barrier()
            AR.reset(); ARr.reset()
            KT = ARr.alloc([2, 2560])
            VV = ARr.alloc([20, 256])
            tok = [AR.alloc([1536]) for _ in range(2)]
            sqk = AR.alloc([1024]); ssq = AR.alloc([8]); qn = AR.alloc([1024]); qr = AR.alloc([1024]); tmp = AR.alloc([512])
            qnb = [qn, AR.alloc([1024])]; qrb = [qr, AR.alloc([1024])]; ssqb = [AR.alloc([8]) for _ in range(2)]
            QT = [ARr.alloc([8, 128]) for _ in range(2)]
            PTt = [ARr.alloc([512]) for _ in range(4)]
            rden = [AR.alloc([512]) for _ in range(2)]; osb = [AR.alloc([512]) for _ in range(2)]
            cnt = 0

            def seq_body(si, t0, T, lat):
                nonlocal cnt
                ntile = T // 128
                nkt = ntile + (4 if lat else 0)
                for i in range(ntile):
                    s = cnt % 2; cnt += 1
                    tk = "tok%d" % s
                    S.load("sp", lambda e, s=s, i=i: e.dma_start(out=tok[s][:, 0:512], in_=PTOK[t0 + i * 128:t0 + (i + 1) * 128, 1024:1536]), tk, ["PTOK"])
                    qn_, qr_, pk = qnb[i % 2], qrb[i % 2], i % 2
                    qk_norm(tok[s][:, 0:256], qn_[:, 0:256], 2, kgrow, sqk, ssqb[pk], tk, "qn%d" % pk, sk="ssq%d" % pk)
                    ksrc, kk = qn_, "qn%d" % pk
                    if not lat:
                        S.store("sp", lambda e, i=i, si=si, qn_=qn_: e.dma_start(out=new_k[si, l, i * 128:(i + 1) * 128, :], in_=qn_[:, 0:256]), "qn%d" % pk)
                        S.store("sp", lambda e, i=i, si=si, s=s: e.dma_start(out=new_v[si, l, i * 128:(i + 1) * 128, :], in_=tok[s][:, 256:512]), tk)
                    else:
                        rope_ops(qn_[:, 0:256], qr_[:, 0:256], tmp, 2, i, "qn%d" % pk, "qr%d" % pk)
                        ksrc, kk = qr_, "qr%d" % pk
                    for kv in range(2):
                        tr(ps[6][:, kv * 128:(kv + 1) * 128], ksrc[:, kv * 128:(kv + 1) * 128], [kk], ["ps6"])
                    evac("dve", KT[:, :, i * 128:(i + 1) * 128], ps[6][:, 0:256].rearrange("p (a b) -> p a b", b=128), ["ps6"], ["KT"])
                    evac("dve", VV[:, i, :], tok[s][:, 256:512], [tk], ["VV"])
                if lat:
                    for j in range(4):
                        s = cnt % 2; cnt += 1
                        tk = "tok%d" % s
                        S.load("sp", lambda e, s=s, j=j: e.dma_start(out=tok[s][:, 0:256], in_=ck[l, j * 128:(j + 1) * 128, :]), tk)
                        S.load("sp", lambda e, s=s, j=j: e.dma_start(out=tok[s][:, 256:512], in_=cv[l, j * 128:(j + 1) * 128, :]), tk)
                        for kv in range(2):
                            tr(ps[6][:, kv * 128:(kv + 1) * 128], tok[s][:, kv * 128:(kv + 1) * 128], [tk], ["ps6"])
                        evac("dve", KT[:, :, T + j * 128:T + (j + 1) * 128], ps[6][:, 0:256].rearrange("p (a b) -> p a b", b=128), ["ps6"], ["KT"])
                        evac("dve", VV[:, ntile + j, :], tok[s][:, 256:512], [tk], ["VV"])
                qst = {}

                def q_pro_a(i):
                    nonlocal cnt
                    s = cnt % 2; cnt += 1
                    tk = "tok%d" % s
                    qst[i] = (s, tk)
                    S.load("sp", lambda e: e.dma_start(out=tok[s][:, 0:1024], in_=PTOK[t0 + i * 128:t0 + (i + 1) * 128, 0:1024]), tk, ["PTOK"])
                    qk_norm(tok[s][:, 0:1024], qnb[i % 2], 8, qgrow, sqk, ssqb[i % 2], tk, "qn%d" % (i % 2), part="a", sk="ssq%d" % (i % 2))

                def q_pro_b(i):
                    s, tk = qst[i]
                    qk_norm(tok[s][:, 0:1024], qnb[i % 2], 8, qgrow, sqk, ssqb[i % 2], tk, "qn%d" % (i % 2), part="b", sk="ssq%d" % (i % 2))
                    if lat:
                        rope_ops(qnb[i % 2], qrb[i % 2], tmp, 8, i, "qn%d" % (i % 2), "qr%d" % (i % 2))

                def q_pro_c(i):
                    qsrc, qk_ = (qrb[i % 2], "qr%d" % (i % 2)) if lat else (qnb[i % 2], "qn%d" % (i % 2))
                    for h2 in range(2):
                        b = 6 + h2
                        for j in range(4):
                            hh = h2 * 4 + j
                            tr(ps[b][:, j * 128:(j + 1) * 128], qsrc[:, hh * 128:(hh + 1) * 128], [qk_], [psk[b]])
                        evac("dve", QT[i % 2][:, h2 * 4:(h2 + 1) * 4, :], ps[b][:, :].rearrange("p (a b) -> p a b", b=128), [psk[b]], ["QT%d" % (i % 2)])

                q_pro_a(0); q_pro_b(0); q_pro_c(0)
                for i in range(ntile):
                    qs = i % 2
                    steps = [(kvg, kt) for kvg in range(2) for kt in range(nkt)]
                    LA = 3
                    SB = (0, 1, 6, 7)

                    def st_mm(n):
                        kvg, kt = steps[n]
                        b = SB[n % 4]
                        mm(ps[b][:, :], KT[:, kvg, kt * 128:(kt + 1) * 128], QT[qs][:, kvg * 4:(kvg + 1) * 4, :].rearrange("p a b -> p (a b)"),
                           True, True, ["KT", "QT%d" % qs], [psk[b]])

                    for n in range(min(LA, len(steps))):
                        st_mm(n)
                    for n, (kvg, kt) in enumerate(steps):
                        b = n % 4
                        sbk = SB[b]
                        oa, dn = 2 + 2 * kvg, 3 + 2 * kvg
                        S.add("act", lambda e, b=b, sbk=sbk: e.activation(out=PTt[b], in_=ps[sbk][:, :], func=AF.Exp, scale=float(128 ** -0.5)), reads=[psk[sbk]], writes=["PT%d" % b])
                        if n + LA < len(steps):
                            st_mm(n + LA)
                        if i + 1 < ntile and n == 1:
                            q_pro_a(i + 1)
                        if i + 1 < ntile and n == min(14, len(steps) - 1):
                            q_pro_b(i + 1)
                        mm(ps[oa][:, :], VV[:, kt, kvg * 128:(kvg + 1) * 128], PTt[b], kt == 0, kt == nkt - 1, ["VV", "PT%d" % b], [psk[oa]])
                        mm(ps[dn][:, :], ones_r, PTt[b], kt == 0, kt == nkt - 1, ["cstr", "PT%d" % b], [psk[dn]])
                        if kt == nkt - 1:
                            ob = kvg
                            S.add("dve", lambda e, ob=ob, dn=dn: e.reciprocal(out=rden[ob], in_=ps[dn][:, :]), reads=[psk[dn]], writes=["rden%d" % ob])
                            S.add("dve", lambda e, ob=ob, oa=oa: e.tensor_tensor(out=osb[ob], in0=ps[oa][:, :], in1=rden[ob], op=ALU.mult), reads=[psk[oa], "rden%d" % ob], writes=["osb%d" % ob])
                            S.store("sp", lambda e, ob=ob, kvg=kvg, i=i: e.dma_start(
                                out=AOS[kvg * 512:(kvg + 1) * 512, t0 + i * 128:t0 + (i + 1) * 128].rearrange("(h p) n -> p h n", p=128),
                                in_=osb[ob].rearrange("p (h n) -> p h n", n=128)), "osb%d" % ob, ["AOS"])
                    if i + 1 < ntile:
                        q_pro_c(i + 1)

            for si in GROUPS[g][2]:
                seq_body(si, *SEQS[si])

        def phaseE(l, g):
            S.barrier()
            AR.reset(); ARr.reset()
            XB = [ARr.alloc([32, 128]) for _ in range(2)]
            Btok = [ARr.alloc([1024]) for _ in range(2)]; xd = [ARr.alloc([2048]) for _ in range(2)]; xdw = [ARr.alloc([2048]) for _ in range(2)]
            Rt = [ARr.alloc([16, 128]) for _ in range(2)]
            Sst2 = [ARr.alloc([2048]) for _ in range(2)]
            dcount = [0]
            xs_tok = AR.alloc([2048]); dtt = [AR.alloc([64]) for _ in range(2)]; dat = [AR.alloc([64]) for _ in range(2)]; dec = [AR.alloc([96]) for _ in range(2)]
            cbm = [AR.alloc([4, 128]) for _ in range(2)]
            Et = [AR.alloc([512]) for _ in range(4)]; Mt = [AR.alloc([512]) for _ in range(4)]
            ych = AR.alloc([2048]); ytmp = [AR.alloc([256]) for _ in range(2)]; zt = AR.alloc([2048]); yft = AR.alloc([2048]); ss1 = AR.alloc([2])
            cnt = 0

            def dir_body(si, t0, T, lat, d):
                    nonlocal cnt
                    nch = T // 128
                    sbi = dcount[0] % 2; dcount[0] += 1
                    Sst = Sst2[sbi]
                    SK = "Sst%d_" % sbi
                    Lend, Lacs = (Mgt, Mle) if d == 0 else (Mlt, Mge)
                    Lseg_r = Mgt_r if d == 0 else Mlt_r
                    tri = Mle if d == 0 else Mge
                    if lat:
                        S.load("sp", lambda e, d=d: e.dma_start(out=yft.rearrange("p (j n) -> p j n", n=128), in_=s0[l, d].rearrange("(j p) n -> p j n", p=128)), "yft")
                        for q in range(4):
                            b = 5 + q % 2
                            for j in range(4):
                                tr(ps[b][:, j * 128:(j + 1) * 128], yft[:, (q * 4 + j) * 128:(q * 4 + j + 1) * 128], ["yft"], [psk[b]])
                            evac("act" if q % 2 == 0 else "dve", Sst[:, q * 512:(q + 1) * 512], ps[b][:, :], [psk[b]], [SK + str(2 * q), SK + str(2 * q + 1)])
                    else:
                        S.add("dve", lambda e: e.tensor_scalar(out=Sst, in0=ngrow, scalar1=0.0, scalar2=None, op0=ALU.mult), reads=["ngrow"], writes=[SK + str(k) for k in range(8)])
                    cnt0 = cnt
                    cnt += nch

                    def issue_loads(cc):
                        c = cc if d == 0 else nch - 1 - cc
                        tc0 = t0 + c * 128
                        s = (cnt0 + cc) % 2
                        S.load("pool", lambda e: e.dma_start(out=XB[s], in_=XBCS[:, tc0:tc0 + 128].rearrange("(fc p) n -> p fc n", p=128)), "XB%d" % s, ["XBCS"])
                        S.load("sp", lambda e: e.dma_start(out=dtt[s], in_=PTOK[tc0:tc0 + 128, 3584:3648]), "dtt%d" % s, ["PTOK"])

                    issue_loads(0)

                    def chunk_body(cc):
                        c = cc if d == 0 else nch - 1 - cc
                        tc0 = t0 + c * 128
                        s = (cnt0 + cc) % 2
                        xk = "XB%d" % s
                        XBc = XB[s]
                        dtt_, dat_, dec_ = dtt[s], dat[s], dec[s]
                        Btok_, xd_, xdw_ = Btok[s], xd[s], xdw[s]
                        kd, ka, ke, kb, kx, kw = "dtt%d" % s, "dat%d" % s, "dec%d" % s, "Btok%d" % s, "xd%d" % s, "xdw%d" % s
                        if d == 1:
                            S.load("sp", lambda e, tc0=tc0: e.dma_start(out=yft, in_=YF[tc0:tc0 + 128, :]), "yft", ["YF"])
                            S.load("sp", lambda e, tc0=tc0: e.dma_start(out=zt, in_=PTOK[tc0:tc0 + 128, 1536:3584]), "zt", ["PTOK"])
                        S.add("dve", lambda e: e.tensor_tensor(out=dtt_, in0=dtt_, in1=dtbrow, op=ALU.add), reads=[kd, "sm"], writes=[kd])
                        S.add("act", lambda e: e.activation(out=dtt_, in_=dtt_, func=AF.Exp), reads=[kd], writes=[kd])
                        S.add("act", lambda e: e.activation(out=dtt_, in_=dtt_, func=AF.Ln, bias=1.0), reads=[kd], writes=[kd])
                        S.add("dve", lambda e: e.tensor_tensor(out=dat_, in0=dtt_, in1=arow, op=ALU.mult), reads=[kd, "arow"], writes=[ka])
                        for half in range(2):
                            S.add("dve" if half == 0 else "pool", lambda e, half=half: e.tensor_tensor(out=Rt[half], in0=bc(tri.unsqueeze(1), [128, 16, 128]),
                                  in1=bc(dat_[:, d * 32 + half * 16:d * 32 + half * 16 + 16].unsqueeze(2), [128, 16, 128]), op=ALU.mult), reads=["cst", ka], writes=["Rt%d" % half])
                        for q in range(4):
                            b = 5 + q % 2
                            for j in range(4):
                                tr(ps[b][:, j * 128:(j + 1) * 128], XBc[:, q * 4 + j, :].bitcast(F32), [xk], [psk[b]])
                            evac("act", xs_tok[:, q * 512:(q + 1) * 512], ps[b][:, :], [psk[b]], ["xs_tok"])
                        for q in range(2):
                            b = 5 + q % 2
                            for j in range(4):
                                tr(ps[b][:, j * 128:(j + 1) * 128], XBc[:, 16 + q * 4 + j, :].bitcast(F32), [xk], [psk[b]])
                            evac("act", Btok_[:, q * 512:(q + 1) * 512], ps[b][:, :], [psk[b]], [kb])
                        dsl = dat_[:, d * 32:(d + 1) * 32]
                        mm(ps[7][:, 0:32], Lend, dsl, True, True, ["cst", ka], ["ps7"])
                        mm(ps[7][:, 32:64], Lacs, dsl, False, True, ["cst", ka], ["ps7"])
                        mm(ps[7][:, 64:96], ones, dsl, False, True, ["cst", ka], ["ps7"])
                        S.add("act", lambda e: e.activation(out=dec_, in_=ps[7][:, 0:96], func=AF.Exp), reads=["ps7"], writes=[ke])
                        v3 = lambda a, h=32: a.rearrange("p (h q) -> p h q", h=h)
                        S.add("dve", lambda e: e.tensor_tensor(out=v3(xd_), in0=v3(xs_tok), in1=bc(dtt_[:, d * 32:(d + 1) * 32].unsqueeze(2), [128, 32, 64]), op=ALU.mult),
                              reads=["xs_tok", kd], writes=[kx])
                        S.add("pool", lambda e: e.tensor_tensor(out=v3(xdw_), in0=v3(xd_), in1=bc(dec_[:, 0:32].unsqueeze(2), [128, 32, 64]), op=ALU.mult),
                              reads=[kx, ke], writes=[kw])
                        if d == 0:
                            S.add("pool", lambda e: e.tensor_tensor(out=v3(yft), in0=v3(xs_tok), in1=bc(drow.unsqueeze(2), [128, 32, 64]), op=ALU.mult),
                                  reads=["xs_tok", "sm"], writes=["yft"])
                        if cc + 1 < nch:
                            issue_loads(cc + 1)
                        for half in range(2):
                            gs = [half * 4 + k for k in range(4)]
                            cb_, kc_ = cbm[half], "cbm%d" % half
                            for k, gi in enumerate(gs):
                                mm(ps[4][:, k * 128:(k + 1) * 128], XBc[:, 16 + gi, :].bitcast(F32), XBc[:, 24 + gi, :].bitcast(F32), k == 0, True, [xk], ["ps4"])
                            S.add("dve", lambda e, cb_=cb_: e.tensor_tensor(out=cb_, in0=ps[4][:, :].rearrange("p (k i) -> p k i", k=4), in1=bc(tri.unsqueeze(1), [128, 4, 128]), op=ALU.mult),
                                  reads=["ps4", "cst"], writes=[kc_])
                            for k, gi in enumerate(gs):
                                mm(ps[k][:, :], Lseg_r, Rt[half][:, k * 4:(k + 1) * 4, :].rearrange("p a b -> p (a b)"), True, True, ["cstr", "Rt%d" % half], [psk[k]])
                            for k, gi in enumerate(gs):
                                S.add("act", lambda e, k=k: e.activation(out=Et[k], in_=ps[k][:, :], func=AF.Exp), reads=[psk[k]], writes=["Et%d" % k])
                            for k, gi in enumerate(gs):
                                S.add("dve" if k % 2 == 0 else "pool", lambda e, k=k, cb_=cb_: e.tensor_tensor(out=Mt[k].rearrange("p (h i) -> p h i", h=4), in0=Et[k].rearrange("p (h i) -> p h i", h=4),
                                      in1=bc(cb_[:, k, :].unsqueeze(1), [128, 4, 128]), op=ALU.mult), reads=["Et%d" % k, kc_], writes=["Mt%d" % k])
                            for k, gi in enumerate(gs):
                                for h in range(4):
                                    hh = gi * 4 + h
                                    mm(ps[k][:, h * 64:(h + 1) * 64], Mt[k][:, h * 128:(h + 1) * 128], xd_[:, hh * 64:(hh + 1) * 64].bitcast(F32), h == 0, True,
                                       ["Mt%d" % k, kx], [psk[k]])
                                mm(ps[k][:, 256:512], XBc[:, 24 + gi, :], Sst[:, gi * 256:(gi + 1) * 256], False, True, [xk, SK + str(gi)], [psk[k]])
                            for k, gi in enumerate(gs):
                                yt_, ky = ytmp[k % 2], "ytmp%d" % (k % 2)
                                S.add("dve", lambda e, k=k, gi=gi, yt_=yt_: e.tensor_tensor(out=yt_.rearrange("p (h q) -> p h q", h=4), in0=ps[k][:, 256:512].rearrange("p (h q) -> p h q", h=4),
                                      in1=bc(dec_[:, 32 + gi * 4:32 + gi * 4 + 4].unsqueeze(2), [128, 4, 64]), op=ALU.mult), reads=[psk[k], ke], writes=[ky])
                                S.add("dve", lambda e, k=k, gi=gi, yt_=yt_: e.tensor_tensor(out=ych[:, gi * 256:(gi + 1) * 256], in0=ps[k][:, 0:256], in1=yt_, op=ALU.add),
                                      reads=[psk[k], ky], writes=["ych"])
                            for k, gi in enumerate(gs):
                                sb = k
                                mm(ps[sb][:, 0:256], Btok_[:, gi * 128:(gi + 1) * 128], xdw_[:, gi * 256:(gi + 1) * 256], True, True, [kb, kw], [psk[sb]])
                            for k, gi in enumerate(gs):
                                sb = k
                                sg_ = Sst[:, gi * 256:(gi + 1) * 256]
                                S.add("pool", lambda e, gi=gi, sg_=sg_: e.tensor_tensor(out=sg_.rearrange("p (h q) -> p h q", h=4), in0=sg_.rearrange("p (h q) -> p h q", h=4),
                                      in1=bc(dec_[:, 64 + gi * 4:64 + gi * 4 + 4].unsqueeze(2), [128, 4, 64]), op=ALU.mult), reads=[SK + str(gi), ke], writes=[SK + str(gi)])
                                S.add("dve", lambda e, sg_=sg_, sb=sb, k=k: e.tensor_tensor(out=sg_, in0=sg_, in1=ps[sb][:, 0:256], op=ALU.add),
                                      reads=[SK + str(gi), psk[sb]], writes=[SK + str(gi)])
                        if d == 0:
                            S.add("dve", lambda e: e.tensor_tensor(out=ych, in0=ych, in1=yft, op=ALU.add), reads=["ych", "yft"], writes=["ych"])
                            S.store("sp", lambda e, tc0=tc0: e.dma_start(out=YF[tc0:tc0 + 128, :], in_=ych), "ych", ["YF"])
                        else:
                            S.add("dve", lambda e: e.tensor_tensor(out=ych, in0=ych, in1=yft, op=ALU.add), reads=["ych", "yft"], writes=["ych"])
                            S.add("pool", lambda e: e.tensor_tensor(out=ych, in0=ych, in1=zt, op=ALU.mult), reads=["ych", "zt"], writes=["ych"])
                            S.add("act", lambda e: e.activation(out=yft, in_=ych, func=AF.Square, accum_out=ss1[:, 0:1]), reads=["ych"], writes=["yft", "ss1"])
                            S.add("act", lambda e: e.activation(out=ss1[:, 0:1], in_=ss1[:, 0:1], func=AF.Sqrt, scale=1.0 / 2048, bias=EPS), reads=["ss1"], writes=["ss1"])
                            S.add("dve", lambda e: e.reciprocal(out=ss1[:, 0:1], in_=ss1[:, 0:1]), reads=["ss1"], writes=["ss1"])
                            S.add("dve", lambda e: e.scalar_tensor_tensor(out=ych, in0=ych, scalar=ss1[:, 0:1], in1=ngrow, op0=ALU.mult, op1=ALU.mult),
                                  reads=["ych", "ss1", "ngrow"], writes=["ych"])
                            for q in range(4):
                                b = 5 + q % 2
                                for j in range(4):
                                    tr(ps[b][:, j * 128:(j + 1) * 128], ych[:, (q * 4 + j) * 128:(q * 4 + j + 1) * 128], ["ych"], [psk[b]])
                                evac("act" if q % 2 == 0 else "dve", zt[:, q * 512:(q + 1) * 512], ps[b][:, :], [psk[b]], ["zt"])
                            S.store("sp", lambda e, tc0=tc0: e.dma_start(out=YST[:, tc0:tc0 + 128].rearrange("(fc p) n -> p fc n", p=128),
                                    in_=zt.rearrange("p (fc n) -> p fc n", n=128)), "zt", ["YST"])
                    for cc in range(nch):
                        chunk_body(cc)
                    if not lat:
                        for q in range(4):
                            b = 5 + q % 2
                            for j in range(4):
                                tr(ps[b][:, j * 128:(j + 1) * 128], Sst[:, (q * 4 + j) * 128:(q * 4 + j + 1) * 128].bitcast(F32), [SK + str(k) for k in range(8)], [psk[b]])
                            evac("act", xs_tok[:, q * 512:(q + 1) * 512], ps[b][:, :], [psk[b]], ["xs_tok"])
                        S.store("sp", lambda e, si=si, d=d: e.dma_start(out=new_ssd[si, l, d].rearrange("(j p) n -> p j n", p=128),
                                in_=xs_tok.rearrange("p (j n) -> p j n", n=128)), "xs_tok")

            for si in GROUPS[g][2]:
                for d in range(2):
                    dir_body(si, *SEQS[si], d)

        def phaseFG(l, g):
            S.barrier()
            AR.reset(); ARr.reset()
            g0, gn, _ = GROUPS[g]
            kind = g
            NT = 1024
            R1 = ARr.alloc([11, NT]); R2 = ARr.alloc([8, NT])
            R1f = R1.rearrange("p a b -> p (a b)")
            BIN = R1f[:, 0:8 * NT].rearrange("p (a b) -> p a b", b=NT)
            sq = R1f[:, 0:4096].rearrange("p (a b) -> p a b", b=512)
            wb = [ARr.alloc([2048]) for _ in range(4)]
            xT = AR.alloc([8, NT]); Gt = [AR.alloc([NT]) for _ in range(2)]; rstd = AR.alloc([512])
            xn = [AR.alloc([512]) for _ in range(2)]; tmp = [AR.alloc([512]) for _ in range(2)]; xtok = [AR.alloc([1024]) for _ in range(2)]
            last = (l == n_layers - 1)
            wcnt = [0]; tcnt = [0]

            WL = []
            wstate = {"issued": 0, "next": 0, "base": 0}
            LA = 2

            def wnext():
                j = wstate["next"]; wstate["next"] += 1
                while wstate["issued"] < min(len(WL), j + LA + 1):
                    k = wstate["issued"]; wstate["issued"] += 1
                    sk = (wstate["base"] + k) % 4
                    for (vf, src) in WL[k]:
                        wload(vf(wb[sk]), src, "wb%d" % sk)
                return (wstate["base"] + j) % 4

            def build_wl():
                WL.clear()
                v8 = lambda n0, n1: (lambda t: t.rearrange("p (kc n) -> p kc n", kc=8)[:, :, n0:n1])
                kcp = lambda a: a.rearrange("(kc p) n -> p kc n", p=128)
                for (src, r0, wsrc, k0, grow0) in units:
                    for blk in range(4):
                        WL.append([(v8(0, 256), kcp(wsrc[l][k0:k0 + 1024, blk * 256:(blk + 1) * 256]))])
                for blk in range(4):
                    WL.append([(v8(0, 256), kcp(w_merge[l][:, blk * 256:(blk + 1) * 256]))])
                for half in range(2):
                    for p0 in range(0, 11, 2):
                        c0 = half * 11 + p0
                        npc = min(2, 11 - p0)
                        WL.append([(v8(0, npc * 128), kcp(ffn_w1[l][:, c0 * 128:(c0 + npc) * 128]))])
                        WL.append([(v8(0, npc * 128), kcp(ffn_w1[l][:, DFF + c0 * 128:DFF + (c0 + npc) * 128]))])
                    for fo in range(8):
                        WL.append([((lambda t: t[:, 0:1408].rearrange("p (kc n) -> p kc n", kc=11)), kcp(ffn_w2[l][half * 1408:(half + 1) * 1408, fo * 128:(fo + 1) * 128]))])
                wstate["base"] = (wstate["base"] + wstate["next"]) % 4
                wstate["issued"] = 0; wstate["next"] = 0

            def nps():
                b = rr["ps"] % 4; rr["ps"] += 1
                return b

            units = ((AOS, 0, w_attn_o, 0, FM_GT), (CVS, 0, w_conv_o, 0, FM_GT + 1024), (YST, 0, w_ssd_o, 0, FM_GT + 2048), (YST, 1024, w_ssd_o, 1024, FM_GT + 2048))
            for t in range(gn // NT):
                t0 = g0 + t * NT
                build_wl()
                load_xT(l, t0, NT, xT, xtok, "xT")
                gcnt = 0
                for ui, (src, r0, wsrc, k0, grow0) in enumerate(units):
                    S.load("pool", lambda e, src=src, r0=r0, t0=t0: e.dma_start(out=BIN, in_=src[r0:r0 + 1024, t0:t0 + NT].rearrange("(kc p) n -> p kc n", p=128)), "R1", ["AOS", "CVS", "YST"])
                    for blk in range(4):
                        s = wnext()
                        wv = wb[s].rearrange("p (kc n) -> p kc n", kc=8)
                        for fo2 in range(2):
                            fa = blk * 2 + fo2
                            gq = gcnt % 2; gcnt += 1
                            S.load("sp", lambda e, gq=gq, fa=fa, grow0=grow0, t0=t0: e.dma_start(out=Gt[gq], in_=PFM[grow0 + fa * 128:grow0 + (fa + 1) * 128, t0:t0 + NT]), "Gt%d" % gq, ["PFM"])
                            for sub in range(2):
                                b = nps()
                                sl = slice(sub * 512, (sub + 1) * 512)
                                for kc in range(8):
                                    mm(ps[b][:, :], wv[:, kc, fo2 * 128:(fo2 + 1) * 128], BIN[:, kc, sl], kc == 0, kc == 7, ["wb%d" % s, "R1"], [psk[b]])
                                if ui == 0:
                                    S.add("dve", lambda e, b=b, fa=fa, sl=sl, gq=gq: e.tensor_tensor(out=R2[:, fa, sl], in0=ps[b][:, :], in1=Gt[gq][:, sl], op=ALU.mult), reads=[psk[b], "Gt%d" % gq], writes=["R2"])
                                else:
                                    tq = tcnt[0] % 2; tcnt[0] += 1
                                    S.add("dve", lambda e, b=b, sl=sl, gq=gq, tq=tq: e.tensor_tensor(out=tmp[tq], in0=ps[b][:, :], in1=Gt[gq][:, sl], op=ALU.mult), reads=[psk[b], "Gt%d" % gq], writes=["tmp%d" % tq])
                                    S.add("pool", lambda e, fa=fa, sl=sl, tq=tq: e.tensor_tensor(out=R2[:, fa, sl], in0=R2[:, fa, sl], in1=tmp[tq], op=ALU.add), reads=["R2", "tmp%d" % tq], writes=["R2"])
                for blk in range(4):
                    s = wnext()
                    wv = wb[s].rearrange("p (kc n) -> p kc n", kc=8)
                    for fo2 in range(2):
                        fa = blk * 2 + fo2
                        for sub in range(2):
                            b = nps()
                            sl = slice(sub * 512, (sub + 1) * 512)
                            for kc in range(8):
                                mm(ps[b][:, :], wv[:, kc, fo2 * 128:(fo2 + 1) * 128], R2[:, kc, sl], kc == 0, kc == 7, ["wb%d" % s, "R2"], [psk[b]])
                            S.add("dve", lambda e, b=b, fa=fa, sl=sl: e.scalar_tensor_tensor(out=xT[:, fa, sl], in0=ps[b][:, :], scalar=G1[:, fa, kind:kind + 1], in1=xT[:, fa, sl], op0=ALU.mult, op1=ALU.add),
                                  reads=[psk[b], "modv", "xT"], writes=["xT"])
                for sub in range(2):
                    sl = slice(sub * 512, (sub + 1) * 512)
                    ln_fm(xT[:, :, sl], R2[:, :, sl], sq, 512, A2, B2, kind, 5, ["xT"], "R2", rstd, xn, sqkey="R1")
                for half in range(2):
                    fcs = list(range(half * 11, half * 11 + 11))
                    for p0 in range(0, 11, 2):
                        pc = fcs[p0:p0 + 2]
                        npc = len(pc)
                        s1 = wnext(); s2 = wnext()
                        wg = wb[s1].rearrange("p (kc n) -> p kc n", kc=8); wu = wb[s2].rearrange("p (kc n) -> p kc n", kc=8)
                        for j, fc in enumerate(pc):
                            fl = fc - half * 11
                            for sub in range(2):
                                sl = slice(sub * 512, (sub + 1) * 512)
                                ba = nps(); bb = nps()
                                for kc in range(8):
                                    mm(ps[ba][:, :], wg[:, kc, j * 128:(j + 1) * 128], R2[:, kc, sl], kc == 0, kc == 7, ["wb%d" % s1, "R2"], [psk[ba]])
                                for kc in range(8):
                                    mm(ps[bb][:, :], wu[:, kc, j * 128:(j + 1) * 128], R2[:, kc, sl], kc == 0, kc == 7, ["wb%d" % s2, "R2"], [psk[bb]])
                                tq = tcnt[0] % 2; tcnt[0] += 1
                                S.add("act", lambda e, ba=ba, tq=tq: e.activation(out=tmp[tq], in_=ps[ba][:, :], func=AF.Silu), reads=[psk[ba]], writes=["tmp%d" % tq])
                                S.add("dve", lambda e, bb=bb, fl=fl, sl=sl, tq=tq: e.tensor_tensor(out=R1[:, fl, sl], in0=tmp[tq], in1=ps[bb][:, :], op=ALU.mult), reads=["tmp%d" % tq, psk[bb]], writes=["R1"])
                    for fo in range(8):
                        s = wnext()
                        wv = wb[s][:, 0:1408].rearrange("p (kc n) -> p kc n", kc=11)
                        for sub in range(2):
                            sl = slice(sub * 512, (sub + 1) * 512)
                            b = nps()
                            for kc in range(11):
                                mm(ps[b][:, :], wv[:, kc, :], R1[:, kc, sl], kc == 0, kc == 10, ["wb%d" % s, "R1"], [psk[b]])
                            S.add("dve", lambda e, b=b, fo=fo, sl=sl: e.scalar_tensor_tensor(out=xT[:, fo, sl], in0=ps[b][:, :], scalar=G2[:, fo, kind:kind + 1], in1=xT[:, fo, sl], op0=ALU.mult, op1=ALU.add),
                                  reads=[psk[b], "modv", "xT"], writes=["xT"])
                dst = y_allT if last else XST
                S.store("sp", lambda e, t0=t0, dst=dst: e.dma_start(out=dst[:, t0:t0 + NT].rearrange("(kc p) n -> p kc n", p=128), in_=xT), "xT", ["XST"])

        PH = {"A": phaseAB, "C": phaseC, "E": phaseE, "F": phaseFG}
        for l in range(n_layers):
            layer_setup(l)
            for g in range(2):
                for p in phases:
                    if p in PH:
                        PH[p](l, g)
        S.emit(st)
    return nc


def _host_consts():
    k = np.arange(128)[:, None]
    i = np.arange(128)[None, :]
    mats = [np.eye(128), (k <= i), (k > i), (k < i), (k >= i), np.ones((128, 128))]
    consts = np.concatenate([m.astype(np.float32) for m in mats], axis=1)
    rows = TS // 64
    row = np.repeat(np.arange(rows, dtype=np.float32), 64)
    col = np.tile(np.arange(64, dtype=np.float32), rows)
    inv = (1.0 / (np.float32(10000.0) ** (np.arange(0, 64, 2, dtype=np.float32) / np.float32(64)))).astype(np.float32)
    ang = np.concatenate([row[:, None] * inv, col[:, None] * inv], axis=-1).astype(np.float32)
    cs = np.cos(ang).astype(np.float32).reshape(16, 128, 64).transpose(1, 0, 2)
    sn = np.sin(ang).astype(np.float32).reshape(16, 128, 64).transpose(1, 0, 2)
    rope = np.ascontiguousarray(np.stack([cs, sn], axis=1)).astype(np.float32)
    return np.ascontiguousarray(consts), rope


def _prep(inp):
    f = lambda a: np.ascontiguousarray(np.asarray(a, dtype=np.float32))
    consts, rope = _host_consts()
    bro = lambda v: np.broadcast_to(np.asarray(v, np.float32)[None, :], (128, v.shape[-1]))
    small = []
    ngrow = []
    for l in range(NL):
        parts = [
            inp["ada_b"][l].reshape(48, 128).T,
            inp["norm1_g"][l].reshape(8, 128).T,
            inp["norm2_g"][l].reshape(8, 128).T,
            inp["conv_w"][l].reshape(3, 8, 128).transpose(2, 1, 0).reshape(128, 24),
            inp["ssd_conv_w"][l].reshape(3, 32, 128).transpose(2, 1, 0).reshape(128, 96),
            inp["ssd_conv_b"][l].reshape(32, 128).T,
            bro(inp["q_norm_g"][l]),
            bro(inp["k_norm_g"][l]),
            bro(inp["ssd_dt_bias"][l].reshape(64)),
            bro(inp["ssd_a_log"][l].reshape(64)),
            bro(inp["ssd_d"][l]),
        ]
        small.append(np.concatenate([np.asarray(p, np.float32) for p in parts], axis=1))
        ngrow.append(bro(inp["ssd_norm_g"][l]))
    small = f(np.stack(small))
    ngrow = f(np.stack(ngrow))
    shared = dict(ada_w=f(inp["ada_w"]), w_in=f(inp["w_in"]), w_attn_o=f(inp["w_attn_o"]), w_conv_o=f(inp["w_conv_o"]),
                  w_ssd_o=f(inp["w_ssd_o"]), w_merge=f(inp["w_merge"]), ffn_w1=f(inp["ffn_w1"]), ffn_w2=f(inp["ffn_w2"]),
                  small=small, ngrow=ngrow, consts=consts, rope=rope)
    maps = []
    for c in range(8):
        m = dict(shared)
        m["x_allT"] = f(np.concatenate([inp["x_prompt"][4 * c:4 * c + 4].reshape(4 * TP, D), inp["x_sample"][c]], axis=0).T)
        m["c_fm"] = f(np.stack([inp["c_ctx"].reshape(8, 128).T, inp["c"][c].reshape(8, 128).T], axis=-1))
        m["ck"] = f(inp["cache_k"][c].reshape(NL, PAST, 256))
        m["cv"] = f(inp["cache_v"][c].reshape(NL, PAST, 256))
        m["s0"] = f(inp["state_ssd"][c].reshape(NL, 2, 2048, 128))
        maps.append(m)
    return maps


_NC_CACHE = {}


def kernel(**inputs):
    inp = {k: np.asarray(v) for k, v in inputs.items()}
    maps = _prep(inp)
    if "nc" not in _NC_CACHE:
        _NC_CACHE["nc"] = build()
    res = run_bass_kernel_spmd(_NC_CACHE["nc"], maps, core_ids=list(range(8)))
    r = res.results
    y_prompt = np.concatenate([np.ascontiguousarray(r[c]["y_allT"].T[:4 * TP]).reshape(4, TP, D) for c in range(8)], axis=0)
    y_sample = np.stack([np.ascontiguousarray(r[c]["y_allT"].T[4 * TP:]) for c in range(8)], axis=0)
    new_k = np.concatenate([r[c]["new_k"].reshape(4, NL, TP, 2, 128) for c in range(8)], axis=0)
    new_v = np.concatenate([r[c]["new_v"].reshape(4, NL, TP, 2, 128) for c in range(8)], axis=0)
    new_ssd = np.concatenate([r[c]["new_ssd"].reshape(4, NL, 2, 32, 64, 128) for c in range(8)], axis=0)
    return (y_prompt.astype(np.float32), y_sample.astype(np.float32), new_k.astype(np.float32),
            new_v.astype(np.float32), new_ssd.astype(np.float32))
```
